# Optimizing a Trainium2 kernel written in Bass

```python
import math
import jax, jax.numpy as jnp
from jax import lax
import numpy as np


D_MODEL = 1024
BATCH = 2
SEQ = 8192
DEPTH = 2

HEAD_DIM = 128
BLOCK = 128
GRID_W = 64
EPS = 1e-6
NEG_INF = -1e30
RET_HEADS = D_MODEL // 256
RET_DK = 128
RET_DV = 256
RET_CHUNK = 128
RET_THETA = 10000.0
SWA_HEADS = D_MODEL // HEAD_DIM
SWA_KV_HEADS = 2
WINDOW = 128
T5_BUCKETS = 32
T5_MAX_DIST = 128
AX_HEADS = D_MODEL // HEAD_DIM
AX_KV_HEADS = 2
AX_THETA = 10000.0
D_FF = 4 * D_MODEL

N_EVEN = (DEPTH + 1) // 2
N_ODD = DEPTH // 2
RET_Q = RET_HEADS * RET_DK
RET_V = RET_HEADS * RET_DV
SWA_Q = SWA_HEADS * HEAD_DIM
SWA_KV = SWA_KV_HEADS * HEAD_DIM
EVEN_IN = 2 * RET_Q + 2 * RET_V + SWA_Q + 2 * SWA_KV
EVEN_OUT = RET_V + SWA_Q
AX_Q = AX_HEADS * HEAD_DIM
AX_KV = AX_KV_HEADS * HEAD_DIM
ODD_IN = AX_Q + 2 * AX_KV

kernel_name = "hybrid_retention_swa_axialrope_encoder"


def rms_norm(x, g):
    xf = x.astype(jnp.float32)
    y = xf * lax.rsqrt(jnp.mean(xf * xf, axis=-1, keepdims=True) + EPS)
    return (y * g.astype(jnp.float32)).astype(x.dtype)


def split_cols(a, sizes):
    offs, o = [], 0
    for s in sizes[:-1]:
        o += s
        offs.append(o)
    return jnp.split(a, offs, axis=-1)


def rope_angles(pos, dim, theta):
    inv = theta ** (-jnp.arange(0, dim, 2, dtype=jnp.float32) / dim)
    return pos.astype(jnp.float32)[:, None] * inv[None, :]


def apply_rope(x, ang):
    d2 = x.shape[-1] // 2
    xf = x.astype(jnp.float32)
    x1, x2 = xf[..., :d2], xf[..., d2:]
    c = jnp.cos(ang)[None, :, None, :]
    s = jnp.sin(ang)[None, :, None, :]
    return jnp.concatenate([x1 * c - x2 * s, x2 * c + x1 * s], axis=-1).astype(x.dtype)


def retention_direction(q, k, v, log_gamma, strict):
    Bn, S, H, dk = q.shape
    dv = v.shape[-1]
    C = RET_CHUNK
    nc = S // C
    qc = q.reshape(Bn, nc, C, H, dk)
    kc = k.reshape(Bn, nc, C, H, dk)
    vc = v.reshape(Bn, nc, C, H, dv)
    idx = jnp.arange(C, dtype=jnp.float32)
    diff = idx[:, None] - idx[None, :]
    mask = (diff > 0) if strict else (diff >= 0)
    decay = jnp.where(mask[None], jnp.exp(jnp.maximum(diff, 0.0)[None] * log_gamma[:, None, None]), 0.0)
    scores = jnp.einsum('bnrhd,bnjhd->bnhrj', qc, kc) * decay[None, None]
    inner = jnp.einsum('bnhrj,bnjhe->bnrhe', scores, vc)
    k_w = kc * jnp.exp((C - 1 - idx)[:, None] * log_gamma[None, :])[None, None, :, :, None]
    U = jnp.einsum('bnjhd,bnjhe->nbhde', k_w, vc)
    chunk_decay = jnp.exp(C * log_gamma)[None, :, None, None]

    def step(state, u):
        return chunk_decay * state + u, state

    _, prev = lax.scan(step, jnp.zeros((Bn, H, dk, dv), jnp.float32), U)
    q_w = qc * jnp.exp((idx + 1)[:, None] * log_gamma[None, :])[None, None, :, :, None]
    cross = jnp.einsum('bnrhd,nbhde->bnrhe', q_w, prev)
    return (inner + cross).reshape(Bn, S, H, dv)


def retention_mixer(q, k, v, g, decay_logit, gn_gain):
    Bn, S, _ = q.shape
    dt = v.dtype
    q = q.reshape(Bn, S, RET_HEADS, RET_DK)
    k = k.reshape(Bn, S, RET_HEADS, RET_DK)
    v = v.reshape(Bn, S, RET_HEADS, RET_DV).astype(jnp.float32)
    ang = rope_angles(jnp.arange(S), RET_DK, RET_THETA)
    q = apply_rope(q, ang).astype(jnp.float32)
    k = apply_rope(k, ang).astype(jnp.float32) * (RET_DK ** -0.5)
    log_gamma = jax.nn.log_sigmoid(decay_logit.astype(jnp.float32))
    fwd = retention_direction(q, k, v, log_gamma[0], False)
    bwd = jnp.flip(retention_direction(jnp.flip(q, 1), jnp.flip(k, 1), jnp.flip(v, 1),
                                       log_gamma[1], True), 1)
    y = rms_norm(fwd + bwd, gn_gain.reshape(RET_HEADS, RET_DV))
    y = y.reshape(Bn, S, RET_V).astype(dt)
    return jax.nn.silu(g) * y


def t5_bucket(rel):
    nb = T5_BUCKETS // 2
    max_exact = nb // 2
    ret = jnp.where(rel > 0, nb, 0)
    n = jnp.abs(rel)
    nf = jnp.maximum(n, 1).astype(jnp.float32)
    large = max_exact + (jnp.log(nf / max_exact) / math.log(T5_MAX_DIST / max_exact)
                         * (nb - max_exact)).astype(jnp.int32)
    large = jnp.minimum(large, nb - 1)
    return ret + jnp.where(n < max_exact, n, large)


def window_attention(q, k, v, sink, t5_table):
    Bn, S, Hq, D = q.shape
    Hkv = k.shape[2]
    G = Hq // Hkv
    nb = S // BLOCK
    pad = ((0, 0), (BLOCK, BLOCK), (0, 0), (0, 0))
    kp = jnp.pad(k, pad).reshape(Bn, nb + 2, BLOCK, Hkv, D)
    vp = jnp.pad(v, pad).reshape(Bn, nb + 2, BLOCK, Hkv, D)
    kw = jnp.concatenate([kp[:, :nb], kp[:, 1:nb + 1], kp[:, 2:]], axis=2)
    vw = jnp.concatenate([vp[:, :nb], vp[:, 1:nb + 1], vp[:, 2:]], axis=2)
    qb = q.reshape(Bn, nb, BLOCK, Hkv, G, D)
    s = jnp.einsum('bnqkgd,bnjkd->bnkgqj', qb, kw,
                   preferred_element_type=jnp.float32) * (D ** -0.5)
    r = jnp.arange(BLOCK)
    j = jnp.arange(3 * BLOCK)
    rel = j[None, :] - BLOCK - r[:, None]
    bias = t5_table.astype(jnp.float32)[t5_bucket(rel)]
    bias = bias.transpose(2, 0, 1).reshape(Hkv, G, BLOCK, 3 * BLOCK)
    kpos = (jnp.arange(nb)[:, None] - 1) * BLOCK + j[None, :]
    valid = (jnp.abs(rel) <= WINDOW)[None] & ((kpos >= 0) & (kpos < S))[:, None, :]
    s = jnp.where(valid[None, :, None, None], s + bias[None, None], NEG_INF)
    sink_l = sink.astype(jnp.float32).reshape(Hkv, G)[None, None, :, :, None, None]
    m = jnp.maximum(jnp.max(s, axis=-1, keepdims=True), sink_l)
    p = jnp.exp(s - m)
    p = p / (jnp.sum(p, axis=-1, keepdims=True) + jnp.exp(sink_l - m))
    o = jnp.einsum('bnkgqj,bnjkd->bnqkgd', p.astype(v.dtype), vw)
    return o.reshape(Bn, S, Hq * D)


def axial_attention(q, k, v):
    Bn, S, Hq, D = q.shape
    Hkv = k.shape[2]
    G = Hq // Hkv
    rows = S // GRID_W
    row = jnp.repeat(jnp.arange(rows), GRID_W)
    col = jnp.tile(jnp.arange(GRID_W), rows)
    half = D // 2
    ang_r = rope_angles(row, half, AX_THETA)
    ang_c = rope_angles(col, half, AX_THETA)
    q = jnp.concatenate([apply_rope(q[..., :half], ang_r), apply_rope(q[..., half:], ang_c)], axis=-1)
    k = jnp.concatenate([apply_rope(k[..., :half], ang_r), apply_rope(k[..., half:], ang_c)], axis=-1)
    nb = S // BLOCK
    qb = q.reshape(Bn, nb, BLOCK, Hkv, G, D).transpose(1, 0, 2, 3, 4, 5)
    scale = D ** -0.5

    def block(qi):
        s = jnp.einsum('bqkgd,bjkd->bkgqj', qi, k, preferred_element_type=jnp.float32) * scale
        p = jax.nn.softmax(s, axis=-1)
        return jnp.einsum('bkgqj,bjkd->bqkgd', p.astype(v.dtype), v)

    o = lax.map(block, qb)
    return o.transpose(1, 0, 2, 3, 4, 5).reshape(Bn, S, Hq * D)


def setup_inputs(seed: int = 0) -> dict:
    key = jax.random.key(seed)
    ks = jax.random.split(key, 20)
    f32 = jnp.float32

    def w(k, shape, fan_in):
        return jax.random.normal(k, shape, f32) * (fan_in ** -0.5)

    def gain(k, shape):
        return 1.0 + 0.05 * jax.random.normal(k, shape, f32)

    base_logit = jnp.log(2.0 ** (5.0 + jnp.arange(RET_HEADS, dtype=f32)) - 1.0)
    return {
        'x': jax.random.normal(ks[0], (BATCH, SEQ, D_MODEL), f32),
        'norm_mix': gain(ks[1], (DEPTH, D_MODEL)),
        'norm_mlp': gain(ks[2], (DEPTH, D_MODEL)),
        'w_in_even': w(ks[3], (N_EVEN, D_MODEL, EVEN_IN), D_MODEL),
        'w_out_even': w(ks[4], (N_EVEN, EVEN_OUT, D_MODEL), EVEN_OUT),
        'ret_decay_logit': base_logit[None, None, :] + 0.1 * jax.random.normal(ks[5], (N_EVEN, 2, RET_HEADS), f32),
        'ret_norm': gain(ks[6], (N_EVEN, RET_V)),
        'swa_q_norm': gain(ks[7], (N_EVEN, HEAD_DIM)),
        'swa_k_norm': gain(ks[8], (N_EVEN, HEAD_DIM)),
        'swa_sink': 0.5 * jax.random.normal(ks[9], (N_EVEN, SWA_HEADS), f32),
        't5_table': 0.5 * jax.random.normal(ks[10], (T5_BUCKETS, SWA_HEADS), f32),
        'w_in_odd': w(ks[11], (N_ODD, D_MODEL, ODD_IN), D_MODEL),
        'w_out_odd': w(ks[12], (N_ODD, AX_Q, D_MODEL), AX_Q),
        'ax_q_norm': gain(ks[13], (N_ODD, HEAD_DIM)),
        'ax_k_norm': gain(ks[14], (N_ODD, HEAD_DIM)),
        'w_mlp_up': w(ks[15], (DEPTH, D_MODEL, D_FF), D_MODEL),
        'w_mlp_down': w(ks[16], (DEPTH, D_FF, D_MODEL), D_FF),
    }


def reference(x, norm_mix, norm_mlp, w_in_even, w_out_even, ret_decay_logit, ret_norm,
              swa_q_norm, swa_k_norm, swa_sink, t5_table, w_in_odd, w_out_odd,
              ax_q_norm, ax_k_norm, w_mlp_up, w_mlp_down):
    Bn, S, _ = x.shape
    for layer in range(DEPTH):
        h = rms_norm(x, norm_mix[layer])
        if layer % 2 == 0:
            i = layer // 2
            proj = h @ w_in_even[i]
            qa, ka, va, ga, qb, kb, vb = split_cols(
                proj, [RET_Q, RET_Q, RET_V, RET_V, SWA_Q, SWA_KV, SWA_KV])
            ya = retention_mixer(qa, ka, va, ga, ret_decay_logit[i], ret_norm[i])
            qb = rms_norm(qb.reshape(Bn, S, SWA_HEADS, HEAD_DIM), swa_q_norm[i])
            kb = rms_norm(kb.reshape(Bn, S, SWA_KV_HEADS, HEAD_DIM), swa_k_norm[i])
            vb = vb.reshape(Bn, S, SWA_KV_HEADS, HEAD_DIM)
            yb = window_attention(qb, kb, vb, swa_sink[i], t5_table)
            y = jnp.concatenate([ya, yb], axis=-1) @ w_out_even[i]
        else:
            i = layer // 2
            proj = h @ w_in_odd[i]
            qc, kc, vc = split_cols(proj, [AX_Q, AX_KV, AX_KV])
            qc = rms_norm(qc.reshape(Bn, S, AX_HEADS, HEAD_DIM), ax_q_norm[i])
            kc = rms_norm(kc.reshape(Bn, S, AX_KV_HEADS, HEAD_DIM), ax_k_norm[i])
            vc = vc.reshape(Bn, S, AX_KV_HEADS, HEAD_DIM)
            y = axial_attention(qc, kc, vc) @ w_out_odd[i]
        x = x + y
        h = rms_norm(x, norm_mlp[layer])
        x = x + jnp.square(jax.nn.relu(h @ w_mlp_up[layer])) @ w_mlp_down[layer]
    return x
```

```python
import math
from contextlib import ExitStack, nullcontext

import numpy as np
import concourse.bass as bass
import concourse.mybir as mybir
from concourse.bass_utils import run_bass_kernel_spmd

F32 = mybir.dt.float32
BF16 = mybir.dt.bfloat16
I32 = mybir.dt.int32
ALU = mybir.AluOpType
AF = mybir.ActivationFunctionType
AX = mybir.AxisListType

EPS = 1e-6
NT = 16
TOK = 2048
D = 1024
PI = math.pi
GROUPS = [[0, 1, 2, 3], [4, 5, 6, 7]]
NEG = -30000.0
EMBED_WAIT = True
ALT_Q = False


class Buf:
    __slots__ = ("name", "w", "r")

    def __init__(self, name=""):
        self.name = name
        self.w = None
        self.r = []


class _Eng:
    def __init__(self, name):
        self.name = name
        self.count = 0
        self.ops = []
        self.seen = {}


class Prog:
    ENGS = ("pe", "act", "dve", "pool", "sp")
    NRING = 12

    def __init__(self, nc):
        self.nc = nc
        self.eng = {n: _Eng(n) for n in self.ENGS}
        self.ring = {q: [0] * self.NRING for q in ("sp", "pool", "act")}
        self.ring_pos = {q: 0 for q in ("sp", "pool", "act")}
        self.cc_count = 0
        self.st = ExitStack()
        self.sems = {}
        for k in self.ENGS:
            self.sems[k] = self.st.enter_context(nc.semaphore("s_" + k))
        for q in ("sp", "pool", "act"):
            for i in range(self.NRING):
                k = "d_%s_%d" % (q, i)
                self.sems[k] = self.st.enter_context(nc.semaphore(k))
        for i in range(4):
            k = "cc%d" % i
            self.sems[k] = self.st.enter_context(nc.semaphore(k))

    def _record(self, eng, fn, reads, writes, ev_key, ev_val, inc, extra_need=()):
        E = self.eng[eng]
        waits = {}

        def need(ev):
            if ev is None:
                return
            k, v = ev
            if k == "pe" and eng == "pe":
                return
            if E.seen.get(k, 0) >= v:
                return
            if waits.get(k, 0) < v:
                waits[k] = v

        for b in reads:
            need(b.w)
        for b in writes:
            need(b.w)
            for ev in b.r:
                need(ev)
        for ev in extra_need:
            need(ev)
        for k, v in waits.items():
            E.seen[k] = v
        ev = (ev_key, ev_val)
        E.ops.append((tuple(waits.items()), fn, ev_key, inc))
        for b in reads:
            b.r.append(ev)
        for b in writes:
            b.w = ev
            b.r = []
        return ev

    def op(self, eng, fn, reads=(), writes=()):
        E = self.eng[eng]
        E.count += 1
        return self._record(eng, fn, reads, writes, eng, E.count, 1)

    def dma(self, q, out, in_, reads=(), writes=()):
        pos = self.ring_pos[q]
        self.ring_pos[q] = (pos + 1) % self.NRING
        key = "d_%s_%d" % (q, pos)
        prev = self.ring[q][pos]
        self.ring[q][pos] = prev + 16
        extra = [(key, prev)] if prev > 0 else []

        def fn(e):
            return e.dma_start(out=out, in_=in_)

        return self._record(q, fn, reads, writes, key, prev + 16, 16, extra)

    def cc(self, fn, reads=(), writes=()):
        key = "cc%d" % self.cc_count
        self.cc_count += 1
        return self._record("pool", fn, reads, writes, key, 1, 1)

    def drain(self, cc=True):
        extra = []
        for q in self.ring:
            for i, v in enumerate(self.ring[q]):
                if v > 0:
                    extra.append(("d_%s_%d" % (q, i), v))
        for i in range(self.cc_count if cc else 0):
            extra.append(("cc%d" % i, 1))
        E = self.eng["sp"]
        E.count += 1
        self._record("sp", lambda e: e.nop(), (), (), "sp", E.count, 1, extra)

    def flush(self):
        self.drain(cc=False)
        nc = self.nc
        sems = self.sems
        self.marks = getattr(self, "marks", [])
        self.marks.append({k: E.count for k, E in self.eng.items()})
        with nc.Block() as block:

            def run(E, e):
                for waits, fn, k, inc in E.ops:
                    if EMBED_WAIT and waits:
                        for wk, wv in waits[1:]:
                            e.wait_ge(sems[wk], wv)
                        ins = fn(e)
                        ins._wait_ge(sems[waits[0][0]], waits[0][1])
                        ins.then_inc(sems[k], inc)
                    else:
                        for wk, wv in waits:
                            e.wait_ge(sems[wk], wv)
                        fn(e).then_inc(sems[k], inc)
                E.ops = []

            @block.tensor
            def _(e):
                run(self.eng["pe"], e)

            @block.scalar
            def _(e):
                run(self.eng["act"], e)

            @block.vector
            def _(e):
                run(self.eng["dve"], e)

            @block.gpsimd
            def _(e):
                run(self.eng["pool"], e)

            @block.sync
            def _(e):
                run(self.eng["sp"], e)


class Pipe:
    def __init__(self):
        self.i = 0
        self.q = []

    def defer(self, k, fn):
        self.q.append((self.i + k, fn))

    def tick(self):
        self.i += 1
        due = [x for x in self.q if x[0] <= self.i]
        self.q = [x for x in self.q if x[0] > self.i]
        for _, fn in due:
            fn()

    def drain(self):
        while self.q:
            self.tick()


class WLoader:
    def __init__(self, mk, st, handle, row0, K, N, stg, q="sp", engs=("pool", "dve", "act")):
        self.mk, self.handle, self.row0, self.N, self.stg, self.q, self.engs = mk, handle, row0, N, stg, q, engs
        KC = K // 128
        self.w, _ = mk.sb(st, [128, KC, N], BF16)
        self.CH = stg.t[0].shape[1]
        assert N % self.CH == 0
        self.todo = [(kc, c0) for c0 in range(0, N, self.CH) for kc in range(KC)]
        self.bufs = {}
        self.pending = []
        self.i = 0

    def step(self, n=1, lag=4):
        mk, N, CH = self.mk, self.N, self.CH
        for _ in range(n):
            if self.todo:
                kc, c0 = self.todo.pop(0)
                t, b = self.stg.next()
                src = bass.AP(self.handle, (self.row0 + kc * 128) * N + c0, [[N, 128], [1, CH]])
                q = self.q
                if q == "sp" and ALT_Q and (len(self.todo) % 2):
                    q = "act"
                mk.dma(q, t[:], src, [], [b])
                self.pending.append((kc, c0, t, b))
            while self.pending and (len(self.pending) > lag or not self.todo):
                kc, c0, t, b = self.pending.pop(0)
                bw = Buf()
                mk.cp(self.engs[self.i % len(self.engs)], self.w[:, kc, c0:c0 + CH], t[:], [b], [bw])
                self.bufs[(kc, c0 // CH)] = bw
                self.i += 1
                if self.todo:
                    break

    def finish(self):
        while self.todo or self.pending:
            self.step(1)

    def wb(self, kc, col):
        key = (kc, col // self.CH)
        while key not in self.bufs:
            self.step(1)
        return self.bufs[key]


class Rot:
    def __init__(self, tensors):
        self.t = tensors
        self.b = [Buf() for _ in tensors]
        self.i = -1

    def next(self):
        self.i += 1
        k = self.i % len(self.t)
        return self.t[k], self.b[k]


class MK:
    def __init__(self, stage=99, dbg=()):
        self.nc = nc = bass.Bass("TRN2", target_bir_lowering=False)
        self.P = Prog(nc)
        self.stage = stage
        self.dbg = set(dbg)
        self.uid = 0
        self.gst = ExitStack()
        self.out_bufs = []

        def din(name, shape):
            return nc.dram_tensor(name, shape, F32, kind="ExternalInput")

        self.x_in = din("x", [TOK + 256, D])
        self.meta_in = din("meta", [128, 8])
        self.norm_mix = din("norm_mix", [2, D])
        self.norm_mlp = din("norm_mlp", [2, D])
        self.w_in_even = din("w_in_even", [D, 4608])
        self.w_out_even = din("w_out_even", [2048, D])
        self.ret_decay = din("ret_decay_logit", [1, 8])
        self.ret_norm = din("ret_norm", [1, D])
        self.swa_q_norm = din("swa_q_norm", [1, 128])
        self.swa_k_norm = din("swa_k_norm", [1, 128])
        self.swa_sink = din("swa_sink", [1, 8])
        self.t5_table = din("t5_table", [32, 8])
        self.w_in_odd = din("w_in_odd", [D, 1536])
        self.w_out_odd = din("w_out_odd", [D, D])
        self.ax_q_norm = din("ax_q_norm", [1, 128])
        self.ax_k_norm = din("ax_k_norm", [1, 128])
        self.w_up = din("w_mlp_up", [2, D, 4096])
        self.w_down = din("w_mlp_down", [2, 4096, D])
        self.y_out = nc.dram_tensor("y", [TOK, D], F32, kind="ExternalOutput")

    def _name(self, p):
        self.uid += 1
        return "%s%d" % (p, self.uid)

    def sb(self, st, shape, dt, n=1):
        ts = [st.enter_context(self.nc.sbuf_tensor(self._name("sb"), list(shape), dt)) for _ in range(n)]
        if n == 1:
            return ts[0], Buf()
        return Rot(ts)

    def ps(self, st, shape, dt, n=1):
        ts = [st.enter_context(self.nc.psum_tensor(self._name("ps"), list(shape), dt)) for _ in range(n)]
        if n == 1:
            return ts[0], Buf()
        return Rot(ts)

    def dram(self, name, shape, dt):
        kind = "ExternalOutput" if name in self.dbg else "Internal"
        return self.nc.dram_tensor(name, list(shape), dt, kind=kind)

    def mm(self, out, lhsT, rhs, start, stop, r, w):
        self.P.op("pe", lambda e: e.matmul(out, lhsT=lhsT, rhs=rhs, start=start, stop=stop), r, w)

    def tr(self, out, in_, r, w):
        ident = self.ident[:]
        self.P.op("pe", lambda e: e.transpose(out=out, in_=in_, identity=ident), list(r) + [self.bI], w)

    def act(self, out, in_, func, r, w, bias=None, scale=None, accum=None):
        kw = {}
        if bias is not None:
            kw["bias"] = bias
        if scale is not None:
            kw["scale"] = scale
        if accum is not None:
            kw["accum_out"] = accum
        self.P.op("act", lambda e: e.activation(out=out, in_=in_, func=func, **kw), r, w)

    def tt(self, eng, out, in0, in1, op, r, w):
        self.P.op(eng, lambda e: e.tensor_tensor(out=out, in0=in0, in1=in1, op=op), r, w)

    def ts(self, eng, out, in0, s1, s2, op0, op1, r, w):
        if op1 is None:
            self.P.op(eng, lambda e: e.tensor_scalar(out=out, in0=in0, scalar1=s1, scalar2=None, op0=op0), r, w)
        else:
            self.P.op(eng, lambda e: e.tensor_scalar(out=out, in0=in0, scalar1=s1, scalar2=s2, op0=op0, op1=op1), r, w)

    def stt(self, eng, out, in0, scalar, in1, op0, op1, r, w):
        self.P.op(eng, lambda e: e.scalar_tensor_tensor(out=out, in0=in0, scalar=scalar, in1=in1, op0=op0, op1=op1), r, w)

    def cp(self, eng, out, in_, r, w):
        if eng == "act":
            self.P.op(eng, lambda e: e.copy(out=out, in_=in_), r, w)
        else:
            self.P.op(eng, lambda e: e.tensor_copy(out=out, in_=in_), r, w)

    def memset(self, eng, ap, val, w):
        self.P.op(eng, lambda e: e.memset(ap, val), [], w)

    def iota(self, out, pattern, base, cm, w):
        self.P.op("pool", lambda e: e.iota(out, pattern=pattern, base=base, channel_multiplier=cm), [], w)

    def recip(self, out, in_, r, w):
        self.P.op("dve", lambda e: e.reciprocal(out=out, in_=in_), r, w)

    def rsum(self, out, in_, r, w):
        self.P.op("dve", lambda e: e.reduce_sum(out=out, in_=in_, axis=AX.X), r, w)

    def dma(self, q, out, in_, r, w):
        self.P.dma(q, out, in_, r, w)

    def bcast_row(self, handle, off, n):
        return bass.AP(handle, off, [[0, 128], [1, n]])

    def rstd(self, st, b, n, inv_count):
        self.ts("dve", st[:, 4:4 + n], st[:, 0:n], inv_count, EPS, ALU.mult, ALU.add, [b], [b])
        self.act(st[:, 8:8 + n], st[:, 4:4 + n], AF.Ln, [b], [b])
        self.act(st[:, 12:12 + n], st[:, 8:8 + n], AF.Exp, [b], [b], scale=-0.5)

    def load_w(self, st, handle, row0, K, N, stg):
        L = WLoader(self, st, handle, row0, K, N, stg)
        self.last_loader = L
        return L.w, L.wb

    def norm_T(self, xt, bx, gain, bg, junk, bj, st, bst, hb, bhb, pT, bpT, hT, bhT):
        self.norm_part(xt, bx, gain, bg, junk, bj, st, bst, hb, bhb)
        self.tr_part(hb, bhb, pT, bpT, hT, bhT)

    def norm_part(self, xt, bx, gain, bg, junk, bj, st, bst, hb, bhb):
        self.act(junk[:], xt[:], AF.Square, [bx], [bj, bst], accum=st[:, 0:1])
        self.rstd(st, bst, 1, 1.0 / D)
        self.stt("dve", hb[:], xt[:], st[:, 12:13], gain[:], ALU.mult, ALU.mult, [bx, bst, bg], [bhb])

    def tr_part(self, hb, bhb, pT, bpT, hT, bhT):
        for kc in range(8):
            self.tr(pT[:, kc * 128:(kc + 1) * 128], hb[:, kc * 128:(kc + 1) * 128], [bhb], [bpT])
        self.cp("act", hT[:], pT[:], [bpT], [bhT])

    def setup(self):
        st = self.gst
        self.ident, self.bI = self.sb(st, [128, 128], BF16)
        self.ones, self.bOnes = self.sb(st, [128, 128], BF16)
        self.meta, self.bMeta = self.sb(st, [128, 8], F32)
        self.lg, self.bLg = self.sb(st, [128, 8], F32)
        self.pcol, self.bPcol = self.sb(st, [128, 1], F32)
        self.negpi, self.bNegpi = self.sb(st, [128, 1], F32)
        self.memset("pool", self.negpi[:], -PI, [self.bNegpi])
        ident = self.ident[:]
        self.memset("pool", ident, 1.0, [self.bI])
        self.P.op("pool", lambda e: e.affine_select(out=ident, in_=ident, pattern=[[-1, 128]],
                                                    compare_op=ALU.is_equal, fill=0.0, base=0,
                                                    channel_multiplier=1), [self.bI], [self.bI])
        self.memset("pool", self.ones[:], 1.0, [self.bOnes])
        self.dma("sp", self.meta[:], self.meta_in.ap(), [], [self.bMeta])
        with nullcontext(st) as lst:
            tmp, bt = self.sb(lst, [128, 8], F32)
            pci, bpci = self.sb(lst, [128, 1], I32)
            self.dma("sp", tmp[:], self.bcast_row(self.ret_decay, 0, 8), [], [bt])
            self.act(tmp[:], tmp[:], AF.Exp, [bt], [bt], scale=-1.0)
            self.act(tmp[:], tmp[:], AF.Ln, [bt], [bt], bias=1.0)
            self.ts("dve", self.lg[:], tmp[:], -1.0, None, ALU.mult, None, [bt], [self.bLg])
            self.iota(pci[:], [[0, 1]], 0, 1, [bpci])
            self.cp("dve", self.pcol[:], pci[:], [bpci], [self.bPcol])

    def sincos(self, st, ang, bang, shape, cos_out, sin_out, bouts):
        C1 = 6.28125
        C2_ = 2 * PI - C1
        sl = tuple(slice(None) for _ in shape)
        t, bt = self.sb(st, shape, F32)
        ki, bki = self.sb(st, shape, I32)
        kf, bkf = self.sb(st, shape, F32)
        r, br = self.sb(st, shape, F32)
        m, bm = self.sb(st, shape, F32)
        self.ts("dve", t[sl], ang, 1.0 / (2 * PI), None, ALU.mult, None, [bang], [bt])
        self.cp("dve", ki[sl], t[sl], [bt], [bki])
        self.cp("dve", kf[sl], ki[sl], [bki], [bkf])
        self.stt("dve", r[sl], kf[sl], -C1, ang, ALU.mult, ALU.add, [bkf, bang], [br])
        self.stt("dve", r[sl], kf[sl], -C2_, r[sl], ALU.mult, ALU.add, [bkf, br], [br])

        def wrap(x, bx):
            self.ts("dve", m[sl], x[sl], PI, None, ALU.is_gt, None, [bx], [bm])
            self.stt("dve", x[sl], m[sl], -2 * PI, x[sl], ALU.mult, ALU.add, [bm, bx], [bx])
            self.ts("dve", m[sl], x[sl], -PI, None, ALU.is_lt, None, [bx], [bm])
            self.stt("dve", x[sl], m[sl], 2 * PI, x[sl], ALU.mult, ALU.add, [bm, bx], [bx])

        wrap(r, br)
        self.act(sin_out, r[sl], AF.Sin, [br], bouts)
        self.ts("dve", t[sl], r[sl], 0.5 * PI, None, ALU.add, None, [br], [bt])
        wrap(t, bt)
        self.act(cos_out, t[sl], AF.Sin, [bt], bouts)

    def phase_A(self):
        nc = self.nc
        self.QT_ret = self.dram("QT_ret", [NT, 128, 512], BF16)
        self.KT_ret = self.dram("KT_ret", [NT, 128, 512], BF16)
        self.KWF = self.dram("KWF", [NT, 128, 512], BF16)
        self.KWB = self.dram("KWB", [NT, 128, 512], BF16)
        self.V_ret = self.dram("V_ret", [NT, 128, 1024], BF16)
        self.G_ret = self.dram("G_ret", [NT, 128, 1024], BF16)
        self.QT_swa = self.dram("QT_swa", [NT, 128, 1024], BF16)
        self.KT_swa = self.dram("KT_swa", [NT + 2, 128, 256], BF16)
        self.V_swa = self.dram("V_swa", [NT + 2, 128, 256], BF16)
        self.bQT_ret = [Buf() for _ in range(NT)]
        self.bKT_ret = [Buf() for _ in range(NT)]
        self.bKWF = [Buf() for _ in range(NT)]
        self.bKWB = [Buf() for _ in range(NT)]
        self.bV_ret = [Buf() for _ in range(NT)]
        self.bG_ret = [Buf() for _ in range(NT)]
        self.bQT_swa = [Buf() for _ in range(NT)]
        self.bKT_swa = [Buf() for _ in range(NT + 2)]
        self.bV_swa = [Buf() for _ in range(NT + 2)]
        with ExitStack() as st:
            stg = self.sb(st, [128, 512], F32, n=6)
            W, bW = self.load_w(st, self.w_in_even, 0, D, 4608, stg)
            gain, bg = self.sb(st, [128, D], F32)
            gng, bgng = self.sb(st, [128, D], F32)
            gq, bgq = self.sb(st, [128, 128], F32)
            gk, bgk = self.sb(st, [128, 128], F32)
            self.dma("sp", gain[:], self.bcast_row(self.norm_mix, 0, D), [], [bg])
            self.dma("sp", gng[:], self.bcast_row(self.ret_norm, 0, D), [], [bgng])
            self.dma("sp", gq[:], self.bcast_row(self.swa_q_norm, 0, 128), [], [bgq])
            self.dma("sp", gk[:], self.bcast_row(self.swa_k_norm, 0, 128), [], [bgk])
            self.ts("dve", gq[:], gq[:], 128.0 ** -0.5, None, ALU.mult, None, [bgq], [bgq])
            gcol, bgcol = self.sb(st, [128, 2], F32)
            self.dma("sp", gcol[:, 0:1], bass.AP(self.swa_q_norm, 0, [[1, 128], [1, 1]]), [], [bgcol])
            self.dma("sp", gcol[:, 1:2], bass.AP(self.swa_k_norm, 0, [[1, 128], [1, 1]]), [bgcol], [bgcol])
            self.ts("dve", gcol[:, 0:1], gcol[:, 0:1], 128.0 ** -0.5, None, ALU.mult, None, [bgcol], [bgcol])
            C2, bC2 = self.sb(st, [128, NT, 128], F32)
            S2, bS2 = self.sb(st, [128, NT, 128], F32)
            kw, bkw = self.sb(st, [128, 8], F32)
            with nullcontext(st) as tst:
                invf, binv = self.sb(tst, [128, 64], F32)
                posi, bposi = self.sb(tst, [128, NT], I32)
                posf, bposf = self.sb(tst, [128, NT], F32)
                ang, bang = self.sb(tst, [128, NT, 64], F32)
                for j in range(64):
                    self.memset("pool", invf[:, j:j + 1], float(np.float32(10000.0) ** np.float32(-2.0 * j / 128.0)), [binv])
                self.iota(posi[:], [[128, NT]], 0, 1, [bposi])
                self.cp("dve", posf[:], posi[:], [bposi], [bposf])
                self.ts("dve", posf[:], posf[:], self.meta[:, 0:1], None, ALU.add, None, [bposf, self.bMeta], [bposf])
                self.tt("dve", ang[:], invf[:].unsqueeze(1).broadcast_to([128, NT, 64]),
                        posf[:].unsqueeze(2).broadcast_to([128, NT, 64]), ALU.mult, [binv, bposf], [bang])
                C2v = C2[:].rearrange("p t (two d) -> p t two d", two=2)
                S2v = S2[:].rearrange("p t (two d) -> p t two d", two=2)
                self.sincos(tst, ang[:], bang, [128, NT, 64], C2v[:, :, 0, :], S2v[:, :, 1, :], [bC2, bS2])
                self.cp("dve", C2v[:, :, 1, :], C2v[:, :, 0, :], [bC2], [bC2])
                self.ts("dve", S2v[:, :, 0, :], S2v[:, :, 1, :], -1.0, None, ALU.mult, None, [bS2], [bS2])
                t127, bt127 = self.sb(tst, [128, 1], F32)
                self.ts("dve", t127[:], self.pcol[:], -1.0, 127.0, ALU.mult, ALU.add, [self.bPcol], [bt127])
                self.ts("dve", kw[:, 0:4], self.lg[:, 0:4], t127[:, 0:1], None, ALU.mult, None, [self.bLg, bt127], [bkw])
                self.ts("dve", kw[:, 4:8], self.lg[:, 4:8], self.pcol[:, 0:1], None, ALU.mult, None, [self.bLg, self.bPcol, bkw], [bkw])
                self.act(kw[:], kw[:], AF.Exp, [bkw], [bkw])
                pass
            X = self.sb(st, [128, D], F32, n=2)
            junk, bj = self.sb(st, [128, D], BF16)
            ST = self.sb(st, [128, 16], F32, n=2)
            HB = self.sb(st, [128, D], BF16, n=2)
            HT = self.sb(st, [128, D], BF16, n=2)
            pT, bpT = self.ps(st, [128, D], BF16)
            PS = self.ps(st, [128, 512], F32, n=5)
            PQ = self.ps(st, [128, 512], BF16, n=2)
            TA = self.sb(st, [128, 512], F32, n=2)
            TB = self.sb(st, [128, 512], F32, n=2)
            QR = self.sb(st, [128, 512], BF16, n=7)
            QTt = self.sb(st, [128, 512], BF16, n=2)
            KW = self.sb(st, [128, 512], BF16, n=4)
            VB = self.sb(st, [128, D], BF16, n=2)
            GB = self.sb(st, [128, D], BF16, n=2)
            QS = self.sb(st, [128, D], BF16, n=2)
            ST4 = self.sb(st, [128, 16], F32, n=3)
            KS = self.sb(st, [128, 256], BF16, n=2)
            VS = self.sb(st, [128, 256], BF16, n=2)
            TA = self.sb(st, [128, 512], F32, n=3)
            TB = self.sb(st, [128, 512], F32, n=3)

            def rope(ps, bps, t, dst, bdst):
                ta, bta = TA.next()
                tb, btb = TB.next()
                qv = ps[:].rearrange("p (h two d) -> p h two d", h=4, two=2)
                Cb = C2[:, t, :].rearrange("p (two d) -> p two d", two=2).unsqueeze(1).broadcast_to([128, 4, 2, 64])
                Sb = S2[:, t, :].rearrange("p (two d) -> p two d", two=2).unsqueeze(1).broadcast_to([128, 4, 2, 64])
                tav = ta[:].rearrange("p (h two d) -> p h two d", h=4, two=2)
                tbv = tb[:].rearrange("p (h two d) -> p h two d", h=4, two=2)
                self.tt("dve", tav, qv, Cb, ALU.mult, [bps, bC2], [bta])
                self.tt("dve", tbv, qv[:, :, ::-1, :], Sb, ALU.mult, [bps, bS2], [btb])
                self.tt("pool", dst[:], ta[:], tb[:], ALU.add, [bta, btb], [bdst])

            def transposes(src, bsrc, n, dst_ap, bdst, scale=None):
                pq, bpq = PQ.next()
                for h in range(n):
                    self.tr(pq[:, h * 128:(h + 1) * 128], src[:, h * 128:(h + 1) * 128], [bsrc], [bpq])
                if scale is None:
                    self.cp("act", dst_ap, pq[:, 0:n * 128], [bpq], [bdst])
                else:
                    self.act(dst_ap, pq[:, 0:n * 128], AF.Copy, [bpq, bgcol], [bdst], scale=scale)

            def headnorm(ps_ap, bps, n, g, bgn, dst_ap, bdst):
                st4, bst4 = ST4.next()
                ta, bta = TA.next()
                tb, btb = TB.next()
                self.act(ta[:, 0:n * 128], ps_ap, AF.Square, [bps], [bta])
                self.rsum(st4[:, 0:n], ta[:, 0:n * 128].rearrange("p (h d) -> p h d", h=n), [bta], [bst4])
                self.rstd(st4, bst4, n, 1.0 / 128)
                self.tt("dve", dst_ap.rearrange("p (h d) -> p h d", h=n),
                        ps_ap.rearrange("p (h d) -> p h d", h=n),
                        st4[:, 12:12 + n].unsqueeze(2).broadcast_to([128, n, 128]), ALU.mult, [bps, bst4], [bdst])

            pipe = Pipe()
            tiles = list(range(NT)) + [NT, NT + 1]

            def start_norm(tt_):
                own = tt_ < NT
                r0 = 128 + tt_ * 128 if own else (0 if tt_ == NT else TOK + 128)
                xt, bx = X.next()
                stt_, bst = ST.next()
                hb, bhb = HB.next()
                self.dma("sp", xt[:], self.x_in.ap()[r0:r0 + 128, :], [], [bx])
                self.norm_part(xt, bx, gain, bg, junk, bj, stt_, bst, hb, bhb)
                return hb, bhb

            def start_tr(hb, bhb):
                hT, bhT = HT.next()
                self.tr_part(hb, bhb, pT, bpT, hT, bhT)
                return hT, bhT

            cur = start_tr(*start_norm(tiles[0]))
            for idx, tt_ in enumerate(tiles):
                own = tt_ < NT
                sidx = tt_ + 1 if own else (0 if tt_ == NT else NT + 1)
                hT, bhT = cur
                groups = list(range(9)) if own else [8]
                vb = gb = qs = None
                for gi, g in enumerate(groups):
                    ps, bps = PS.next()
                    for kc in range(8):
                        self.mm(ps[:], hT[:, kc * 128:(kc + 1) * 128], W[:, kc, g * 512:(g + 1) * 512],
                                kc == 0, kc == 7, [bhT, bW(kc, g * 512)], [bps])
                    if g == 0 or g == 1:
                        qr, bqr = QR.next()
                        rope(ps, bps, tt_, qr, bqr)

                        def stage2(g=g, qr=qr, bqr=bqr, tt_=tt_):
                            qT, bqT = QTt.next()
                            transposes(qr, bqr, 4, qT[:], bqT)
                            if g == 0:
                                self.dma("pool", self.QT_ret.ap()[tt_], qT[:], [bqT], [self.bQT_ret[tt_]])
                            else:
                                self.dma("pool", self.KT_ret.ap()[tt_], qT[:], [bqT], [self.bKT_ret[tt_]])
                        pipe.defer(3, stage2)
                        if g == 1:
                            for d_, (dst, bdst) in enumerate(((self.KWF, self.bKWF), (self.KWB, self.bKWB))):
                                kwt, bkwt = KW.next()
                                self.tt("dve", kwt[:].rearrange("p (h d) -> p h d", h=4),
                                        qr[:].rearrange("p (h d) -> p h d", h=4),
                                        kw[:, d_ * 4:d_ * 4 + 4].unsqueeze(2).broadcast_to([128, 4, 128]),
                                        ALU.mult, [bqr, bkw], [bkwt])
                                self.dma("pool", dst.ap()[tt_], kwt[:], [bkwt], [bdst[tt_]])
                    elif g in (2, 3):
                        if g == 2:
                            vb, bvb = VB.next()
                        self.cp("act", vb[:, (g - 2) * 512:(g - 1) * 512], ps[:], [bps], [bvb])
                        if g == 3:
                            self.dma("pool", self.V_ret.ap()[tt_], vb[:], [bvb], [self.bV_ret[tt_]])
                    elif g in (4, 5):
                        if g == 4:
                            gb, bgb = GB.next()
                        ta, bta = TA.next()
                        self.act(ta[:], ps[:], AF.Silu, [bps], [bta])
                        self.tt("dve", gb[:, (g - 4) * 512:(g - 3) * 512], ta[:], gng[:, (g - 4) * 512:(g - 3) * 512],
                                ALU.mult, [bta, bgng], [bgb])
                        if g == 5:
                            self.dma("pool", self.G_ret.ap()[tt_], gb[:], [bgb], [self.bG_ret[tt_]])
                    elif g in (6, 7):
                        if g == 6:
                            qs, bqs = QS.next()
                        qr, bqr = QR.next()
                        headnorm(ps[:], bps, 4, gq, bgq, qr[:], bqr)

                        def stage2(g=g, qr=qr, bqr=bqr, qs=qs, bqs=bqs, tt_=tt_):
                            transposes(qr, bqr, 4, qs[:, (g - 6) * 512:(g - 5) * 512], bqs, scale=gcol[:, 0:1])
                            if g == 7:
                                self.dma("pool", self.QT_swa.ap()[tt_], qs[:], [bqs], [self.bQT_swa[tt_]])
                        pipe.defer(4, stage2)
                    else:
                        qr, bqr = QR.next()
                        headnorm(ps[:, 0:256], bps, 2, gk, bgk, qr[:, 0:256], bqr)

                        def stage2(qr=qr, bqr=bqr, sidx=sidx):
                            ks, bks = KS.next()
                            transposes(qr, bqr, 2, ks[:], bks, scale=gcol[:, 1:2])
                            self.dma("pool", self.KT_swa.ap()[sidx], ks[:], [bks], [self.bKT_swa[sidx]])
                        pipe.defer(4, stage2)
                        vs, bvs = VS.next()
                        self.cp("act", vs[:], ps[:, 256:512], [bps], [bvs])
                        self.dma("pool", self.V_swa.ap()[sidx], vs[:], [bvs], [self.bV_swa[sidx]])
                    if gi == 0 and idx + 1 < len(tiles):
                        nhb = start_norm(tiles[idx + 1])
                    if gi == min(5, len(groups) - 1) and idx + 1 < len(tiles):
                        nxt = start_tr(*nhb)
                    pipe.tick()
                cur = nxt
            pipe.drain()
            self.P.flush()

    def load_tiles_split(self, st, handle, n, cols, bufs, g=4, queues=("sp", "act"), order=None):
        t, _ = self.sb(st, [128, n, cols], BF16)
        tb = [None] * n
        groups = list(range(0, n, g))
        if order == "desc":
            groups = groups[::-1]
        for i, t0 in enumerate(groups):
            b = Buf()
            self.dma(queues[i % len(queues)], t[:, t0:t0 + g, :], handle.ap()[t0:t0 + g].rearrange("t p c -> p t c"),
                     bufs[t0:t0 + g], [b])
            for k in range(t0, min(t0 + g, n)):
                tb[k] = b
        return t, tb

    def load_tiles(self, st, handle, n, cols, bufs, q="sp"):
        t, b = self.sb(st, [128, n, cols], BF16)
        self.dma(q, t[:], handle.ap().rearrange("t p c -> p t c"), bufs, [b])
        return t, b

    def state_step(self, psU, bpsU, kwf, bKW, vf, bV, state, bstate, gC, k0):
        for h in range(4):
            pu, bpu = psU[h // 2], bpsU[h // 2]
            self.mm(pu[:, (h % 2) * 256:(h % 2 + 1) * 256], kwf(h), vf(h), True, True, [bKW, bV], [bpu])
        for h in range(4):
            pu, bpu = psU[h // 2], bpsU[h // 2]
            self.stt("dve", state[:, k0 + h, :], state[:, k0 + h, :], gC[:, k0 + h:k0 + h + 1],
                     pu[:, (h % 2) * 256:(h % 2 + 1) * 256], ALU.mult, ALU.add,
                     [bstate[k0 + h], bpu, self.bGC], [bstate[k0 + h]])

    def phase_B1(self):
        self.st_src = self.dram("st_src", [8 * 128, 256], F32)
        self.st_all = self.dram("st_all", [4 * 8 * 128, 256], F32)
        self.b_st_src = Buf()
        self.b_st_all = Buf()
        self.gC, self.bGC = self.sb(self.gst, [128, 8], F32)
        self.act(self.gC[:], self.lg[:], AF.Exp, [self.bLg], [self.bGC], scale=128.0)
        with ExitStack() as st:
            KWF, bKWF = self.load_tiles_split(st, self.KWF, NT, 512, self.bKWF)
            KWB, bKWB = self.load_tiles_split(st, self.KWB, NT, 512, self.bKWB, order="desc")
            V, bV = self.load_tiles_split(st, self.V_ret, NT, 1024, self.bV_ret, g=2)
            state, _ = self.sb(st, [128, 8, 256], F32)
            bstate = [Buf() for _ in range(8)]
            self.memset("pool", state[:], 0.0, bstate)
            psU = [self.ps(st, [128, 512], F32) for _ in range(2)]
            pU, bpU = [p[0] for p in psU], [p[1] for p in psU]
            for c in range(NT):
                self.state_step(pU, bpU, lambda h, c=c: KWF[:, c, h * 128:(h + 1) * 128], bKWF[c],
                                lambda h, c=c: V[:, c, h * 256:(h + 1) * 256], bV[c], state, bstate, self.gC, 0)
            for c in reversed(range(NT)):
                self.state_step(pU, bpU, lambda h, c=c: KWB[:, c, h * 128:(h + 1) * 128], bKWB[c],
                                lambda h, c=c: V[:, c, h * 256:(h + 1) * 256], bV[c], state, bstate, self.gC, 4)
            self.dma("sp", self.st_src.ap().rearrange("(k d) e -> d k e", d=128), state[:], bstate, [self.b_st_src])
            src, dst = self.st_src.ap(), self.st_all.ap()
            self.P.cc(lambda e: e.collective_compute("AllGather", ALU.bypass, replica_groups=GROUPS,
                                                     ins=[src], outs=[dst]), [self.b_st_src], [self.b_st_all])
            self.P.flush()

    def phase_B2(self, loader=None):
        self.YbT = self.dram("YbT", [NT, 128, 1024], BF16)
        self.bYbT = [Buf() for _ in range(NT)]
        TBd = self.dram("TBd", [8, 768], F32)
        bTBd = Buf()
        with ExitStack() as st:
            BT, bBT = self.sb(st, [128, 3 * 8 * 128], F32)
            esink, bes = self.sb(st, [128, 8], F32)
            offs, boffs = self.sb(st, [128, 2], F32)
            self.dma("sp", esink[:], self.bcast_row(self.swa_sink, 0, 8), [], [bes])
            self.act(esink[:], esink[:], AF.Exp, [bes], [bes])
            self.ts("dve", offs[:], self.meta[:, 2:4], -1.0, -NEG, ALU.add, ALU.mult, [self.bMeta], [boffs])
            with nullcontext(st) as tst:
                NM = 3 * 255
                reli, b0 = self.sb(tst, [32, NM], I32)
                rel, b1 = self.sb(tst, [32, NM], F32)
                n, b2 = self.sb(tst, [32, NM], F32)
                a, b3 = self.sb(tst, [32, NM], F32)
                tmp, b4 = self.sb(tst, [32, NM], F32)
                oh, b5 = self.sb(tst, [32, NM], F32)
                t5, b6 = self.sb(tst, [32, 8], F32)
                tb, b7 = self.sb(tst, [8, 768], F32)
                J, bJ = self.sb(tst, [128, 128], F32)
                H, bH = self.sb(tst, [128, 3 * 8 * 128], F32)
                pb, bpb = self.ps(tst, [128, 512], F32)
                self.dma("sp", t5[:], self.t5_table.ap(), [], [b6])
                self.iota(reli[:].rearrange("p (a m) -> p a m", a=3), [[128, 3], [-1, 255]], -1, 0, [b0])
                self.cp("dve", rel[:], reli[:], [b0], [b1])
                self.stt("dve", n[:], rel[:], -1.0, rel[:], ALU.mult, ALU.max, [b1], [b2])
                self.ts("dve", a[:], n[:], 8.0, None, ALU.min, None, [b2], [b3])
                for thr in (12, 16, 23, 32, 46, 64, 91):
                    self.stt("dve", a[:], n[:], float(thr), a[:], ALU.is_ge, ALU.add, [b2, b3], [b3])
                self.ts("dve", tmp[:], rel[:], 0.0, 16.0, ALU.is_gt, ALU.mult, [b1], [b4])
                self.tt("dve", a[:], a[:], tmp[:], ALU.add, [b3, b4], [b3])
                self.ts("dve", oh[:], a[:], self.pcol[0:32, 0:1], None, ALU.is_equal, None, [b3, self.bPcol], [b5])
                self.ts("dve", n[:], n[:], 128.0, None, ALU.is_le, None, [b2], [b2])
                self.ts("dve", tmp[:], n[:], -1.0, -NEG, ALU.add, ALU.mult, [b2, b4], [b4])
                for c0, c1 in ((0, 512), (512, NM)):
                    self.mm(pb[0:8, 0:c1 - c0], t5[:, :], oh[:, c0:c1], True, True, [b6, b5], [bpb])
                    self.tt("dve", tb[:, c0:c1], pb[0:8, 0:c1 - c0], n[0:8, c0:c1], ALU.mult, [bpb, b2], [b7])
                    self.tt("dve", tb[:, c0:c1], tb[:, c0:c1], tmp[0:8, c0:c1], ALU.add, [b7, b4], [b7])
                self.dma("sp", TBd.ap()[:, 0:NM], tb[:, 0:NM], [b7], [bTBd])
                for jt in range(3):
                    self.dma("sp", H[:, jt * 1024:(jt + 1) * 1024].rearrange("p (h q) -> p h q", h=8),
                             bass.AP(TBd, jt * 255, [[1, 128], [768, 8], [1, 128]]), [bTBd], [bH])
                self.memset("pool", J[:], 1.0, [bJ])
                Jap = J[:]
                self.P.op("pool", lambda e: e.affine_select(out=Jap, in_=Jap, pattern=[[1, 128]],
                                                            compare_op=ALU.is_equal, fill=0.0, base=-127,
                                                            channel_multiplier=1), [bJ], [bJ])
                for i in range(6):
                    self.mm(pb[:], J[:], H[:, i * 512:(i + 1) * 512], True, True, [bJ, bH], [bpb])
                    self.cp("dve", BT[:, i * 512:(i + 1) * 512], pb[:], [bpb], [bBT])
                pass
            BTb, bBTb = self.sb(st, [128, 3 * 8 * 128], BF16)
            self.cp("dve", BTb[:], BT[:], [bBT], [bBTb])
            BTv = BTb[:].rearrange("p (a h q) -> p a (h q)", a=3, h=8)
            KT, bKT = self.load_tiles(st, self.KT_swa, NT + 2, 256, self.bKT_swa)
            V, bV = self.load_tiles(st, self.V_swa, NT + 2, 256, self.bV_swa)
            QT = self.sb(st, [128, 1024], BF16, n=2)
            PSS = self.ps(st, [128, 512], F32, n=3)
            PD = self.ps(st, [128, 512], F32, n=2)
            PO = self.ps(st, [128, 512], F32, n=2)
            Pb = self.sb(st, [128, 512], BF16, n=5)
            DN = self.sb(st, [128, 512], F32, n=2)
            YB = self.sb(st, [128, 1024], BF16, n=2)
            LA = 2
            onesr, bonesr = self.sb(st, [1, 128], F32)
            esr, besr = self.sb(st, [1, 1024], F32)
            self.memset("pool", onesr[:], 1.0, [bonesr])
            self.cp("dve", esr[:].rearrange("p (h q) -> p h q", h=8), esink[0:1, :].unsqueeze(2).broadcast_to([1, 8, 128]),
                    [bes], [besr])
            its = [(t, kvh, jt) for t in range(NT) for kvh in range(2) for jt in range(3)]
            qts, ybs, units = {}, {}, {}
            pend = []
            tails = []

            def front(t, kvh, jt):
                if kvh == 0 and jt == 0:
                    qt, bqt = QT.next()
                    self.dma("sp", qt[:], self.QT_swa.ap()[t], [self.bQT_swa[t]], [bqt])
                    qts[t] = (qt, bqt)
                    ybs[t] = YB.next()
                qt, bqt = qts[t]
                sT, bsT = PSS.next()
                self.mm(sT[:], self.ident[:], BTv[:, jt, kvh * 512:(kvh + 1) * 512], True, False, [self.bI, bBTb], [bsT])
                self.mm(sT[:], KT[:, t + jt, kvh * 128:(kvh + 1) * 128], qt[:, kvh * 512:(kvh + 1) * 512],
                        False, True, [bKT, bqt], [bsT])
                p_, bp = Pb.next()
                if t == 0 and jt == 0:
                    self.act(p_[:], sT[:], AF.Exp, [bsT, boffs], [bp], bias=offs[:, 0:1])
                elif t == NT - 1 and jt == 2:
                    self.act(p_[:], sT[:], AF.Exp, [bsT, boffs], [bp], bias=offs[:, 1:2])
                else:
                    self.act(p_[:], sT[:], AF.Exp, [bsT], [bp])
                return p_, bp

            def back(i, t, kvh, jt, p_, bp):
                if jt == 0:
                    units[(t, kvh)] = (PD.next(), PO.next())
                (den, bden), (o, bo) = units[(t, kvh)]
                if jt == 0:
                    self.mm(den[:], onesr[:], esr[:, kvh * 512:(kvh + 1) * 512], True, False, [bonesr, besr], [bden])
                self.mm(den[:], self.ones[:], p_[:], False, jt == 2, [self.bOnes, bp], [bden])
                self.mm(o[:], V[:, t + jt, kvh * 128:(kvh + 1) * 128], p_[:], jt == 0, jt == 2, [bV, bp], [bo])
                if jt == 2:
                    dn, bdn = DN.next()
                    yb, byb = ybs[t]
                    self.recip(dn[:], den[:], [bden], [bdn])
                    self.tt("dve", yb[:, kvh * 512:(kvh + 1) * 512], o[:], dn[:], ALU.mult, [bo, bdn], [byb])
                    if kvh == 1:
                        self.dma("pool", self.YbT.ap()[t], yb[:], [byb], [self.bYbT[t]])

            n = len(its)
            for i in range(n + LA):
                if i < n:
                    pend.append(front(*its[i]))
                if i >= LA:
                    p_, bp = pend[i - LA]
                    back(i, *its[i - LA], p_, bp)
                if loader is not None and i % 2 == 0:
                    loader.step(1)
            self.P.flush()

    def phase_B3(self, loader=None):
        self.SBd = self.dram("SBd", [NT, 128, 1024], BF16)
        self.bSBd = [Buf() for _ in range(NT)]
        self.xres = self.dram("xres", [NT, 128, D], F32)
        self.bxres = [Buf() for _ in range(NT)]
        SC = 128.0 ** -0.5
        with ExitStack() as st:
            S_in, _ = self.sb(st, [128, 8, 256], F32)
            bS = [Buf() for _ in range(8)]
            with ExitStack() as tst:
                V, bV = self.load_tiles_split(tst, self.V_ret, NT, 1024, self.bV_ret, g=2, order="desc")
                SBT = self.sb(tst, [128, 1024], BF16, n=2)
                ri, b0 = self.sb(tst, [128, 4], I32)
                rv, b1 = self.sb(tst, [128, 4], F32)
                ev, b2 = self.sb(tst, [128, 8], F32)
                msk, b3 = self.sb(tst, [128, 8], F32)
                coef, b4 = self.sb(tst, [128, 8, 4], F32)
                ip, b5 = self.sb(tst, [128, 2], F32)
                self.iota(ri[:], [[1, 4]], 0, 0, [b0])
                self.cp("dve", rv[:], ri[:], [b0], [b1])
                self.ts("dve", ip[:, 0:1], self.meta[:, 1:2], -1.0, None, ALU.add, None, [self.bMeta], [b5])
                self.ts("dve", ip[:, 1:2], self.meta[:, 1:2], 1.0, None, ALU.add, None, [self.bMeta, b5], [b5])
                self.ts("dve", ev[:, 0:4], rv[:], -1.0, ip[:, 0:1], ALU.mult, ALU.add, [b1, b5], [b2])
                self.ts("dve", ev[:, 4:8], rv[:], ip[:, 1:2], None, ALU.subtract, None, [b1, b5, b2], [b2])
                self.ts("dve", msk[:], ev[:], 0.0, None, ALU.is_ge, None, [b2], [b3])
                self.ts("dve", ev[:], ev[:], 0.0, None, ALU.max, None, [b2], [b2])
                for d_ in range(2):
                    for r in range(4):
                        k = d_ * 4 + r
                        self.ts("dve", coef[:, k, :], self.lg[:, d_ * 4:d_ * 4 + 4], ev[:, k:k + 1], float(TOK),
                                ALU.mult, ALU.mult, [self.bLg, b2, b4], [b4])
                self.act(coef[:], coef[:], AF.Exp, [b4], [b4])
                for k in range(8):
                    self.ts("dve", coef[:, k, :], coef[:, k, :], msk[:, k:k + 1], None, ALU.mult, None, [b4, b3], [b4])
                FA = self.sb(tst, [128, 4, 256], F32, n=2)
                for k in range(8):
                    d_, h = k // 4, k % 4
                    fa, bfa = FA.next()
                    self.dma("sp", fa[:], bass.AP(self.st_all, k * 128 * 256, [[256, 128], [8 * 128 * 256, 4], [1, 256]]),
                             [self.b_st_all], [bfa])
                    for r in range(4):
                        cf = coef[:, d_ * 4 + r, h:h + 1]
                        if r == 0:
                            self.ts("dve", S_in[:, k, :], fa[:, 0, :], cf, None, ALU.mult, None, [bfa, b4], [bS[k]])
                        else:
                            self.stt("dve", S_in[:, k, :], fa[:, r, :], cf, S_in[:, k, :], ALU.mult, ALU.add, [bfa, b4, bS[k]], [bS[k]])
                KWB, bKWB = self.load_tiles_split(tst, self.KWB, NT, 512, self.bKWB, order="desc")
                psU = [self.ps(tst, [128, 512], F32) for _ in range(2)]
                pU, bpU = [p[0] for p in psU], [p[1] for p in psU]
                for c in reversed(range(NT)):
                    sbt, bsbt = SBT.next()
                    self.cp("act", sbt[:], S_in[:, 4:8, :].rearrange("p h e -> p (h e)"), bS[4:8], [bsbt])
                    self.dma("pool", self.SBd.ap()[c], sbt[:], [bsbt], [self.bSBd[c]])
                    self.state_step(pU, bpU, lambda h, c=c: KWB[:, c, h * 128:(h + 1) * 128], bKWB[c],
                                    lambda h, c=c: V[:, c, h * 256:(h + 1) * 256], bV[c], S_in, bS, self.gC, 4)
                self.P.flush()
            DT, bDT = self.sb(st, [128, 4, 128], F32)
            WQ, bWQ = self.sb(st, [128, 8, 128], F32)
            with nullcontext(st) as tst:
                di, b0 = self.sb(tst, [128, 128], I32)
                df, b1 = self.sb(tst, [128, 128], F32)
                pos, b2 = self.sb(tst, [128, 128], F32)
                neg, b3 = self.sb(tst, [128, 128], F32)
                arg, b4 = self.sb(tst, [128, 128], F32)
                r1i, b5 = self.sb(tst, [128, 128], I32)
                r1, b6 = self.sb(tst, [128, 128], F32)
                r2, b7 = self.sb(tst, [128, 128], F32)
                self.iota(di[:], [[1, 128]], 0, -1, [b0])
                self.cp("dve", df[:], di[:], [b0], [b1])
                self.ts("dve", pos[:], df[:], 0.0, None, ALU.max, None, [b1], [b2])
                self.tt("dve", neg[:], pos[:], df[:], ALU.subtract, [b1, b2], [b3])
                for h in range(4):
                    self.ts("dve", arg[:], pos[:], self.lg[:, h:h + 1], None, ALU.mult, None, [b2, self.bLg, b4], [b4])
                    self.stt("dve", arg[:], neg[:], self.lg[:, 4 + h:5 + h], arg[:], ALU.mult, ALU.add, [b3, self.bLg, b4], [b4])
                    self.act(DT[:, h, :], arg[:], AF.Exp, [b4], [bDT])
                self.ts("dve", DT[:], DT[:], SC, None, ALU.mult, None, [bDT], [bDT])
                self.iota(r1i[:], [[1, 128]], 1, 0, [b5])
                self.cp("dve", r1[:], r1i[:], [b5], [b6])
                self.ts("dve", r2[:], r1[:], -1.0, 129.0, ALU.mult, ALU.add, [b6], [b7])
                for h in range(4):
                    self.act(WQ[:, h, :], r1[:], AF.Exp, [b6, self.bLg], [bWQ], scale=self.lg[:, h:h + 1])
                    self.act(WQ[:, 4 + h, :], r2[:], AF.Exp, [b7, self.bLg], [bWQ], scale=self.lg[:, 4 + h:5 + h])
                self.ts("dve", WQ[:], WQ[:], SC, None, ALU.mult, None, [bWQ], [bWQ])
                pass
            stg = self.sb(st, [128, 512], F32, n=6)
            Wo, bWo = self.load_w(st, self.w_out_even, 0, 2048, D, stg)
            state, bstate = S_in, bS
            VT = self.sb(st, [128, 1024], BF16, n=2)
            SBt = self.sb(st, [128, 1024], BF16, n=2)
            KWt = self.sb(st, [128, 512], BF16, n=2)
            QT = self.sb(st, [128, 512], BF16, n=2)
            KT = self.sb(st, [128, 512], BF16, n=2)
            G = self.sb(st, [128, D], BF16, n=2)
            X = self.sb(st, [128, D], F32, n=3)
            YB = self.sb(st, [128, D], BF16, n=3)
            AT, bAT = self.sb(st, [128, 512], BF16)
            QF, bQF = self.sb(st, [128, 512], BF16)
            QB, bQB = self.sb(st, [128, 512], BF16)
            SF, bSF = self.sb(st, [128, 1024], BF16)
            junk, bj = self.sb(st, [128, D], F32)
            st4, bst4 = self.sb(st, [128, 16], F32)
            t1, bt1 = self.sb(st, [128, D], F32)
            ya, bya = self.sb(st, [128, D], BF16)
            yaT, byaT = self.sb(st, [128, D], BF16)
            XN = self.sb(st, [128, D], F32, n=2)
            psS, bpsS = self.ps(st, [128, 512], F32)
            psY = [self.ps(st, [128, 512], F32) for _ in range(2)]
            psU = [self.ps(st, [128, 512], F32) for _ in range(2)]
            pU, bpU = [p[0] for p in psU], [p[1] for p in psU]
            pT, bpT = self.ps(st, [128, D], BF16)
            psO = [self.ps(st, [128, 512], F32) for _ in range(2)]
            YA = self.sb(st, [128, D], BF16, n=2)
            loaded = {}
            aux = {}

            def S(c):
                qt, bqt = QT.next()
                kt, bkt = KT.next()
                g, bg = G.next()
                xt, bx = X.next()
                yb, byb = YB.next()
                self.dma("sp", qt[:], self.QT_ret.ap()[c], [self.bQT_ret[c]], [bqt])
                self.dma("sp", kt[:], self.KT_ret.ap()[c], [self.bKT_ret[c]], [bkt])
                self.dma("sp", g[:], self.G_ret.ap()[c], [self.bG_ret[c]], [bg])
                self.dma("sp", xt[:], self.x_in.ap()[128 + c * 128:256 + c * 128, :], [], [bx])
                self.dma("sp", yb[:], self.YbT.ap()[c], [self.bYbT[c]], [byb])
                vt, bvt = VT.next()
                sbt, bsbt = SBt.next()
                kwt, bkwt = KWt.next()
                self.dma("sp", vt[:], self.V_ret.ap()[c], [self.bV_ret[c]], [bvt])
                self.dma("sp", sbt[:], self.SBd.ap()[c], [self.bSBd[c]], [bsbt])
                self.dma("sp", kwt[:], self.KWF.ap()[c], [self.bKWF[c]], [bkwt])
                aux[c] = (vt, bvt, sbt, bsbt, kwt, bkwt)
                for h in range(4):
                    self.mm(psS[:, h * 128:(h + 1) * 128], kt[:, h * 128:(h + 1) * 128], qt[:, h * 128:(h + 1) * 128],
                            True, True, [bkt, bqt], [bpsS])
                loaded[c] = (qt, bqt, g, bg, xt, bx, yb, byb)

            def A1(c):
                qt, bqt, g, bg, xt, bx, yb, byb = loaded[c]
                self.tt("dve", AT[:], psS[:], DT[:].rearrange("p h r -> p (h r)"), ALU.mult, [bpsS, bDT], [bAT])
                self.tt("pool", QF[:], qt[:], WQ[:, 0:4, :].rearrange("p h r -> p (h r)"), ALU.mult, [bqt, bWQ], [bQF])
                self.tt("pool", QB[:], qt[:], WQ[:, 4:8, :].rearrange("p h r -> p (h r)"), ALU.mult, [bqt, bWQ], [bQB])
                self.cp("act", SF[:], state[:, 0:4, :].rearrange("p h e -> p (h e)"), bstate[0:4], [bSF])

            def A2(c):
                qt, bqt, g, bg, xt, bx, yb, byb = loaded[c]
                vt, bvt, sbt, bsbt, kwt, bkwt = aux.pop(c)
                if loader is not None:
                    loader.engs = ("act",)
                    loader.step(2)
                for h in range(4):
                    py, bpy = psY[h // 2]
                    yh = py[:, (h % 2) * 256:(h % 2 + 1) * 256]
                    self.mm(yh, AT[:, h * 128:(h + 1) * 128], vt[:, h * 256:(h + 1) * 256], True, False, [bAT, bvt], [bpy])
                    self.mm(yh, QF[:, h * 128:(h + 1) * 128], SF[:, h * 256:(h + 1) * 256], False, False, [bQF, bSF], [bpy])
                    self.mm(yh, QB[:, h * 128:(h + 1) * 128], sbt[:, h * 256:(h + 1) * 256], False, True, [bQB, bsbt], [bpy])
                self.state_step(pU, bpU, lambda h: kwt[:, h * 128:(h + 1) * 128], bkwt,
                                lambda h: vt[:, h * 256:(h + 1) * 256], bvt, state, bstate, self.gC, 0)
                if c >= 1:
                    Btr(c - 1)
                for b_ in range(2):
                    py, bpy = psY[b_]
                    self.act(junk[:, b_ * 512:(b_ + 1) * 512], py[:], AF.Square, [bpy], [bj])
                self.rsum(st4[:, 0:4], junk[:].rearrange("p (h e) -> p h e", h=4), [bj], [bst4])
                self.rstd(st4, bst4, 4, 1.0 / 256)
                for b_ in range(2):
                    py, bpy = psY[b_]
                    self.tt("dve", t1[:, b_ * 512:(b_ + 1) * 512].rearrange("p (h e) -> p h e", h=2),
                            py[:].rearrange("p (h e) -> p h e", h=2),
                            st4[:, 12 + 2 * b_:14 + 2 * b_].unsqueeze(2).broadcast_to([128, 2, 256]), ALU.mult,
                            [bpy, bst4], [bt1])
                ya, bya = YA.next()
                self.tt("dve", ya[:], t1[:], g[:], ALU.mult, [bt1, bg], [bya])
                loaded[c] = loaded[c] + (ya, bya)

            def Btr(c):
                ya, bya = loaded[c][8], loaded[c][9]
                for kc in range(8):
                    self.tr(pT[:, kc * 128:(kc + 1) * 128], ya[:, kc * 128:(kc + 1) * 128], [bya], [bpT])
                self.cp("act", yaT[:], pT[:], [bpT], [byaT])

            def Bout(c):
                qt, bqt, g, bg, xt, bx, yb, byb, ya, bya = loaded.pop(c)
                xn, bxn = XN.next()
                for dg in range(2):
                    po, bpo = psO[dg]
                    for kc in range(16):
                        lhsT = yaT[:, kc * 128:(kc + 1) * 128] if kc < 8 else yb[:, (kc - 8) * 128:(kc - 7) * 128]
                        self.mm(po[:], lhsT, Wo[:, kc, dg * 512:(dg + 1) * 512], kc == 0, kc == 15,
                                [byaT, byb, bWo(kc, dg * 512)], [bpo])
                    self.tt("dve", xn[:, dg * 512:(dg + 1) * 512], po[:], xt[:, dg * 512:(dg + 1) * 512], ALU.add,
                            [bpo, bx], [bxn])
                self.dma("pool", self.xres.ap()[c], xn[:], [bxn], [self.bxres[c]])

            S(0)
            A1(0)
            for c in range(NT):
                A2(c)
                if c + 1 < NT:
                    S(c + 1)
                    A1(c + 1)
                if c >= 1:
                    Bout(c - 1)
            Btr(NT - 1)
            Bout(NT - 1)
            if loader is not None:
                loader.finish()
            self.P.flush()

    def phase_mlp(self, layer, src, bsrc, dst_ap_fn, bdst, pre_up=None, stg=None):
        with ExitStack() as st:
            if stg is None:
                stg = self.sb(st, [128, 512], F32, n=6)
            if pre_up is not None:
                Wu, bWu = pre_up.w, pre_up.wb
            else:
                Wu, bWu = self.load_w(st, self.w_up, layer * D, D, 4096, stg)
            Wd, bWd = self.load_w(st, self.w_down, layer * 4096, 4096, D, stg)
            Ld = self.last_loader
            Ld.engs = ("pool", "act")
            gain, bg = self.sb(st, [128, D], F32)
            self.dma("sp", gain[:], self.bcast_row(self.norm_mlp, layer * D, D), [], [bg])
            X = self.sb(st, [128, D], F32, n=2)
            junk, bj = self.sb(st, [128, D], BF16)
            ST = self.sb(st, [128, 16], F32, n=2)
            HB = self.sb(st, [128, D], BF16, n=2)
            HT = self.sb(st, [128, D], BF16, n=2)
            R = self.sb(st, [128, 512], F32, n=2)
            U, bU = self.sb(st, [128, 4096], BF16)
            UT, bUT = self.sb(st, [128, 4096], BF16)
            XN = self.sb(st, [128, D], F32, n=2)
            pT, bpT = self.ps(st, [128, D], BF16)
            PU = self.ps(st, [128, 512], F32, n=3)
            PT = self.ps(st, [128, D], BF16, n=2)
            psO = [self.ps(st, [128, 512], F32) for _ in range(2)]
            bUTq = [Buf() for _ in range(4)]
            bUq = [Buf() for _ in range(4)]

            def start_norm(t):
                xt, bx = X.next()
                stt_, bst = ST.next()
                hb, bhb = HB.next()
                self.dma("sp", xt[:], src.ap()[t], [bsrc[t]], [bx])
                self.norm_part(xt, bx, gain, bg, junk, bj, stt_, bst, hb, bhb)
                return xt, bx, hb, bhb

            def start_tr(xt, bx, hb, bhb):
                hT, bhT = HT.next()
                self.tr_part(hb, bhb, pT, bpT, hT, bhT)
                return xt, bx, hT, bhT

            cur = start_tr(*start_norm(0))
            for t in range(NT):
                xt, bx, hT, bhT = cur
                pipe = Pipe()
                for fg in range(8):
                    pu, bpu = PU.next()
                    for kc in range(8):
                        self.mm(pu[:], hT[:, kc * 128:(kc + 1) * 128], Wu[:, kc, fg * 512:(fg + 1) * 512],
                                kc == 0, kc == 7, [bhT, bWu(kc, fg * 512)], [bpu])
                    r, br = R.next()
                    self.act(r[:], pu[:], AF.Relu, [bpu], [br])
                    self.tt("dve", U[:, fg * 512:(fg + 1) * 512], r[:], r[:], ALU.mult, [br], [bUq[fg // 2]])
                    if fg % 2 == 1:
                        def trs(q4=fg // 2):
                            pt, bpt = PT.next()
                            for i in range(8):
                                fc = q4 * 8 + i
                                self.tr(pt[:, i * 128:(i + 1) * 128], U[:, fc * 128:(fc + 1) * 128], [bUq[q4]], [bpt])
                            self.cp("act" if q4 % 2 else "dve", UT[:, q4 * 1024:(q4 + 1) * 1024], pt[:], [bpt], [bUTq[q4]])
                        pipe.defer(3, trs)
                    if fg == 0 and t + 1 < NT:
                        nh = start_norm(t + 1)
                    if t == 0:
                        Ld.step(8)
                    pipe.tick()
                if t + 1 < NT:
                    nxt = start_tr(*nh)
                pipe.drain()
                xn, bxn = XN.next()
                for dg in range(2):
                    po, bpo = psO[dg]
                    for fc in range(32):
                        self.mm(po[:], UT[:, fc * 128:(fc + 1) * 128], Wd[:, fc, dg * 512:(dg + 1) * 512],
                                fc == 0, fc == 31, [bUTq[fc // 8], bWd(fc, dg * 512)], [bpo])
                    self.tt("dve", xn[:, dg * 512:(dg + 1) * 512], po[:], xt[:, dg * 512:(dg + 1) * 512], ALU.add,
                            [bpo, bx], [bxn])
                self.dma("pool", dst_ap_fn(t), xn[:], [bxn], [bdst[t]])
                cur = nxt if t + 1 < NT else None
            self.P.flush()

    def phase_E(self):
        self.QT1 = self.dram("QT1", [NT, 128, 1024], BF16)
        self.bQT1 = [Buf() for _ in range(NT)]
        self.k_src = self.dram("k_src", [256, TOK], BF16)
        self.k_all = self.dram("k_all", [4 * 256, TOK], BF16)
        self.v_src = self.dram("v_src", [256, TOK], BF16)
        self.v_all = self.dram("v_all", [4 * 256, TOK], BF16)
        self.b_k_src, self.b_k_all, self.b_v_src, self.b_v_all = Buf(), Buf(), Buf(), Buf()
        with ExitStack() as st:
            stg = self.sb(st, [128, 512], F32, n=6)
            W, bW = self.load_w(st, self.w_in_odd, 0, D, 1536, stg)
            gain, bg = self.sb(st, [128, D], F32)
            gq, bgq = self.sb(st, [128, 128], F32)
            gk, bgk = self.sb(st, [128, 128], F32)
            self.dma("sp", gain[:], self.bcast_row(self.norm_mix, D, D), [], [bg])
            self.dma("sp", gq[:], self.bcast_row(self.ax_q_norm, 0, 128), [], [bgq])
            self.dma("sp", gk[:], self.bcast_row(self.ax_k_norm, 0, 128), [], [bgk])
            self.ts("dve", gq[:], gq[:], 128.0 ** -0.5, None, ALU.mult, None, [bgq], [bgq])
            TBL = {}
            for nm in ("Cq", "Sq", "Ck", "Sk"):
                TBL[nm] = self.sb(st, [128, NT, 128], F32)
            with ExitStack() as tst:
                C4, bC4 = self.sb(tst, [128, NT, 128], F32)
                S4, bS4 = self.sb(tst, [128, NT, 128], F32)
                gqs, bgqs = self.sb(tst, [128, 128], F32)
                gks, bgks = self.sb(tst, [128, 128], F32)
                inv, binv = self.sb(tst, [128, 32], F32)
                rowi, b0 = self.sb(tst, [128, NT], I32)
                rowf, b1 = self.sb(tst, [128, NT], F32)
                pge, b2 = self.sb(tst, [128, 2], F32)
                ang, bang = self.sb(tst, [128, NT, 2, 32], F32)
                cs, bcs = self.sb(tst, [128, NT, 2, 32], F32)
                sn, bsn = self.sb(tst, [128, NT, 2, 32], F32)
                for j in range(32):
                    self.memset("pool", inv[:, j:j + 1], float(np.float32(10000.0) ** np.float32(-2.0 * j / 64.0)), [binv])
                self.iota(rowi[:], [[2, NT]], 0, 0, [b0])
                self.cp("dve", rowf[:], rowi[:], [b0], [b1])
                self.ts("dve", pge[:, 0:1], self.pcol[:], 64.0, None, ALU.is_ge, None, [self.bPcol], [b2])
                self.stt("dve", pge[:, 1:2], pge[:, 0:1], -64.0, self.pcol[:], ALU.mult, ALU.add, [b2, self.bPcol], [b2])
                self.stt("dve", pge[:, 0:1], self.meta[:, 0:1], 1.0 / 64, pge[:, 0:1], ALU.mult, ALU.add, [self.bMeta, b2], [b2])
                self.ts("dve", rowf[:], rowf[:], pge[:, 0:1], None, ALU.add, None, [b1, b2], [b1])
                self.tt("dve", ang[:, :, 0, :], inv[:].unsqueeze(1).broadcast_to([128, NT, 32]),
                        rowf[:].unsqueeze(2).broadcast_to([128, NT, 32]), ALU.mult, [binv, b1], [bang])
                self.ts("dve", ang[:, :, 1, :], inv[:].unsqueeze(1).broadcast_to([128, NT, 32]), pge[:, 1:2], None,
                        ALU.mult, None, [binv, b2, bang], [bang])
                self.sincos(tst, ang[:], bang, [128, NT, 2, 32], cs[:], sn[:], [bcs, bsn])
                C4v = C4[:].rearrange("p t (k two d) -> p t k two d", k=2, two=2)
                S4v = S4[:].rearrange("p t (k two d) -> p t k two d", k=2, two=2)
                for hf in range(2):
                    self.cp("dve", C4v[:, :, :, hf, :], cs[:], [bcs, bsn], [bC4])
                self.cp("dve", S4v[:, :, :, 1, :], sn[:], [bcs, bsn], [bS4])
                self.ts("dve", S4v[:, :, :, 0, :], sn[:], -1.0, None, ALU.mult, None, [bcs, bsn, bS4], [bS4])
                for g_, gs_, bg_, bgs_, cn, sn_ in ((gq, gqs, bgq, bgqs, "Cq", "Sq"), (gk, gks, bgk, bgks, "Ck", "Sk")):
                    self.cp("dve", gs_[:].rearrange("p (k two d) -> p k two d", k=2, two=2),
                            g_[:].rearrange("p (k two d) -> p k two d", k=2, two=2)[:, :, ::-1, :], [bg_], [bgs_])
                    self.tt("dve", TBL[cn][0][:], C4[:], g_[:].unsqueeze(1).broadcast_to([128, NT, 128]), ALU.mult,
                            [bC4, bg_], [TBL[cn][1]])
                    self.tt("dve", TBL[sn_][0][:], S4[:], gs_[:].unsqueeze(1).broadcast_to([128, NT, 128]), ALU.mult,
                            [bS4, bgs_], [TBL[sn_][1]])
                self.P.flush()
            X = self.sb(st, [128, D], F32, n=2)
            junk, bj = self.sb(st, [128, D], BF16)
            ST = self.sb(st, [128, 16], F32, n=2)
            HB = self.sb(st, [128, D], BF16, n=2)
            HT = self.sb(st, [128, D], BF16, n=2)
            pT, bpT = self.ps(st, [128, D], BF16)
            PS = self.ps(st, [128, 512], F32, n=4)
            PQ = self.ps(st, [128, 512], BF16, n=2)
            TA = self.sb(st, [128, 512], F32, n=4)
            TB = self.sb(st, [128, 512], F32, n=4)
            QN = self.sb(st, [128, 512], F32, n=3)
            ST4 = self.sb(st, [128, 16], F32, n=4)
            QR = self.sb(st, [128, 512], BF16, n=8)
            QS = self.sb(st, [128, D], BF16, n=4)
            KS = self.sb(st, [128, 256], BF16, n=2)
            VS = self.sb(st, [128, 256], BF16, n=2)

            def headnorm(ps_ap, bps, n, g, bgn, dst_ap, bdst):
                st4, bst4 = ST4.next()
                ta, bta = TA.next()
                tb, btb = TB.next()
                self.act(ta[:, 0:n * 128], ps_ap, AF.Square, [bps], [bta])
                self.rsum(st4[:, 0:n], ta[:, 0:n * 128].rearrange("p (h d) -> p h d", h=n), [bta], [bst4])
                self.rstd(st4, bst4, n, 1.0 / 128)
                self.tt("dve", dst_ap.rearrange("p (h d) -> p h d", h=n),
                        ps_ap.rearrange("p (h d) -> p h d", h=n),
                        st4[:, 12:12 + n].unsqueeze(2).broadcast_to([128, n, 128]), ALU.mult, [bps, bst4], [bdst])

            def rope(src, bsrc, n, t, dst, bdst, cn="Cq", sn_="Sq"):
                C4, bC4 = TBL[cn]
                S4, bS4 = TBL[sn_]
                ta, bta = TA.next()
                tb, btb = TB.next()
                for k in range(2):
                    sv = src[:, 0:n * 128].rearrange("p (h k two d) -> p h k two d", h=n, k=2, two=2)[:, :, k, :, :]
                    av = ta[:, 0:n * 128].rearrange("p (h k two d) -> p h k two d", h=n, k=2, two=2)[:, :, k, :, :]
                    bv = tb[:, 0:n * 128].rearrange("p (h k two d) -> p h k two d", h=n, k=2, two=2)[:, :, k, :, :]
                    Cb = C4[:, t, k * 64:(k + 1) * 64].rearrange("p (two d) -> p two d", two=2).unsqueeze(1).broadcast_to([128, n, 2, 32])
                    Sb = S4[:, t, k * 64:(k + 1) * 64].rearrange("p (two d) -> p two d", two=2).unsqueeze(1).broadcast_to([128, n, 2, 32])
                    self.tt("dve", av, sv, Cb, ALU.mult, [bsrc, bC4], [bta])
                    self.tt("dve", bv, sv[:, :, ::-1, :], Sb, ALU.mult, [bsrc, bS4], [btb])
                self.tt("pool", dst[:, 0:n * 128], ta[:, 0:n * 128], tb[:, 0:n * 128], ALU.add, [bta, btb], [bdst])

            def transposes(src, bsrc, n, dst_ap, bdst):
                pq, bpq = PQ.next()
                for h in range(n):
                    self.tr(pq[:, h * 128:(h + 1) * 128], src[:, h * 128:(h + 1) * 128], [bsrc], [bpq])
                self.cp("act", dst_ap, pq[:, 0:n * 128], [bpq], [bdst])

            pipe = Pipe()

            def start_norm(t):
                xt, bx = X.next()
                stt_, bst = ST.next()
                hb, bhb = HB.next()
                self.dma("sp", xt[:], self.xres2.ap()[t], [self.bxres2[t]], [bx])
                self.norm_part(xt, bx, gain, bg, junk, bj, stt_, bst, hb, bhb)
                return hb, bhb

            def start_tr(hb, bhb):
                hT, bhT = HT.next()
                self.tr_part(hb, bhb, pT, bpT, hT, bhT)
                return hT, bhT

            HTA, _ = self.sb(st, [128, NT, D], BF16)
            bHTA = [Buf() for _ in range(NT)]

            def start_tr_all(t, hb, bhb):
                self.tr_part(hb, bhb, pT, bpT, HTA[:, t, :], bHTA[t])

            def kv_group(t):
                hT, bhT = HTA[:, t, :], bHTA[t]
                ps, bps = PS.next()
                for kc in range(8):
                    self.mm(ps[:], hT[:, kc * 128:(kc + 1) * 128], W[:, kc, 1024:1536],
                            kc == 0, kc == 7, [bhT, bW(kc, 1024)], [bps])
                qn, bqn = QN.next()
                qr, bqr = QR.next()
                headnorm(ps[:, 0:256], bps, 2, gk, bgk, qn[:, 0:256], bqn)
                rope(qn, bqn, 2, t, qr, bqr, "Ck", "Sk")

                def stage2(qr=qr, bqr=bqr, t=t):
                    ks, bks = KS.next()
                    transposes(qr, bqr, 2, ks[:], bks)
                    self.dma("pool", bass.AP(self.k_src, t * 128, [[TOK, 128], [128 * TOK, 2], [1, 128]]),
                             ks[:].rearrange("p (k q) -> p k q", k=2), [bks], [self.b_k_src])
                pipe.defer(3, stage2)
                vs, bvs = VS.next()
                self.cp("act", vs[:], ps[:, 256:512], [bps], [bvs])
                self.dma("pool", bass.AP(self.v_src, t * 128 * 256, [[256, 128], [1, 256]]),
                         vs[:], [bvs], [self.b_v_src])

            start_tr_all(0, *start_norm(0))
            nh = start_norm(1)
            for t in range(NT):
                kv_group(t)
                if t + 1 < NT:
                    start_tr_all(t + 1, *nh)
                    if t + 2 < NT:
                        nh = start_norm(t + 2)
                pipe.tick()
            pipe.drain()
            ksrc, kdst = self.k_src.ap(), self.k_all.ap()
            vsrc, vdst = self.v_src.ap(), self.v_all.ap()
            self.P.cc(lambda e: e.collective_compute("AllGather", ALU.bypass, replica_groups=GROUPS,
                                                     ins=[ksrc], outs=[kdst]), [self.b_k_src], [self.b_k_all])
            self.P.cc(lambda e: e.collective_compute("AllGather", ALU.bypass, replica_groups=GROUPS,
                                                     ins=[vsrc], outs=[vdst]), [self.b_v_src], [self.b_v_all])
            for t in range(NT):
                hT, bhT = HTA[:, t, :], bHTA[t]
                qs, bqs = QS.next()
                for g in range(2):
                    ps, bps = PS.next()
                    for kc in range(8):
                        self.mm(ps[:], hT[:, kc * 128:(kc + 1) * 128], W[:, kc, g * 512:(g + 1) * 512],
                                kc == 0, kc == 7, [bhT, bW(kc, g * 512)], [bps])
                    qn, bqn = QN.next()
                    qr, bqr = QR.next()
                    headnorm(ps[:], bps, 4, gq, bgq, qn[:], bqn)
                    rope(qn, bqn, 4, t, qr, bqr)

                    def stage2(g=g, qr=qr, bqr=bqr, qs=qs, bqs=bqs, t=t):
                        transposes(qr, bqr, 4, qs[:, g * 512:(g + 1) * 512], bqs)
                        if g == 1:
                            self.dma("pool", self.QT1.ap()[t], qs[:], [bqs], [self.bQT1[t]])
                    pipe.defer(4, stage2)
                    pipe.tick()
            pipe.drain()
            self.P.flush()

    def phase_F(self, loaders=(), wo=None):
        self.YT1 = self.dram("YT1", [NT, 128, 1024], BF16)
        self.bYT1 = [Buf() for _ in range(NT)]
        if wo is not None:
            self.xres3 = self.dram("xres3", [NT, 128, D], F32)
            self.bxres3 = [Buf() for _ in range(NT)]
        LA = 2
        VW = 132
        with ExitStack() as st:
            KT, _ = self.sb(st, [128, 2, 4 * TOK], BF16)
            V, _ = self.sb(st, [128, 64, 2, VW], BF16)
            bKTr = [Buf() for _ in range(4)]
            bVr = [[Buf(), Buf()] for _ in range(4)]
            for r in range(4):
                for kvh in range(2):
                    self.memset("pool", V[:, r * NT:(r + 1) * NT, kvh, 128:VW], 1.0, [bVr[r][kvh]])
            for r in range(4):
                self.dma("sp", KT[:, :, r * TOK:(r + 1) * TOK],
                         bass.AP(self.k_all, r * 256 * TOK, [[TOK, 128], [128 * TOK, 2], [1, TOK]]), [self.b_k_all], [bKTr[r]])
                for kvh in range(2):
                    self.dma("act", V[:, r * NT:(r + 1) * NT, kvh, 0:128],
                             bass.AP(self.v_all, r * 256 * TOK + kvh * 128, [[256, 128], [128 * 256, NT], [1, 128]]),
                             [self.b_v_all], [bVr[r][kvh]])
            QT = self.sb(st, [128, 1024], BF16, n=2)
            YTK = self.sb(st, [128, 512], BF16, n=2)
            YB = self.sb(st, [128, 1024], BF16, n=2)
            RD = self.sb(st, [128, 8], F32, n=2)
            OS = self.sb(st, [128, 4, 132], F32, n=2)
            OSB = [[Buf() for _ in range(4)] for _ in range(2)]
            PSS = self.ps(st, [128, 512], F32, n=3)
            PO4 = [self.ps(st, [128, 512], F32) for _ in range(4)]
            PTf, bPTf = self.ps(st, [128, 512], F32)
            PTb = PTf[:].bitcast(BF16)
            XR = self.sb(st, [128, D], F32, n=2)
            XN = self.sb(st, [128, D], F32, n=2)
            xrs = {}
            Pb = self.sb(st, [128, 512], BF16, n=5)
            its = [(t, kvh, jt) for t in range(NT) for kvh in range(2) for jt in range(64)]
            qts, ybs, units = {}, {}, {}
            pend = []
            tails = []

            def front(t, kvh, jt):
                if kvh == 0 and jt == 0:
                    qt, bqt = QT.next()
                    self.dma("sp", qt[:], self.QT1.ap()[t], [self.bQT1[t]], [bqt])
                    qts[t] = (qt, bqt)
                    ybs[t] = YB.next()
                    if wo is not None:
                        xr, bxr = XR.next()
                        self.dma("sp", xr[:], self.xres2.ap()[t], [self.bxres2[t]], [bxr])
                        xrs[t] = (xr, bxr)
                qt, bqt = qts[t]
                sT, bsT = PSS.next()
                self.mm(sT[:], KT[:, kvh, jt * 128:(jt + 1) * 128], qt[:, kvh * 512:(kvh + 1) * 512],
                        True, True, [bKTr[jt // 16], bqt], [bsT])
                p_, bp = Pb.next()
                self.act(p_[:], sT[:], AF.Exp, [bsT], [bp])
                return p_, bp

            def back(i, t, kvh, jt, p_, bp):
                for h in range(4):
                    o, bo = PO4[h]
                    self.mm(o[:, 0:129], p_[:, h * 128:(h + 1) * 128], V[:, jt, kvh, 0:129],
                            jt == 0, jt == 63, [bVr[jt // 16][kvh], bp], [bo])
                if jt == 63:
                    rd, brd = RD.next()
                    ytk, bytk = YTK.next()
                    yb, byb = ybs[t]
                    os_, _ = OS.next()
                    bosh = OSB[OS.i % 2]
                    for h in range(4):
                        o, bo = PO4[h]
                        self.cp("dve" if h < 3 else "act", os_[:, h, 0:129], o[:, 0:129], [bo], [bosh[h]])
                    self.recip(rd[:, 0:4], os_[:, :, 128], bosh, [brd])
                    for h in range(4):
                        self.ts("dve", ytk[:, h * 128:(h + 1) * 128], os_[:, h, 0:128], rd[:, h:h + 1], None,
                                ALU.mult, None, [bosh[h], brd], [bytk])

                    def tail(ytk=ytk, bytk=bytk, yb=yb, byb=byb, kvh=kvh, t=t):
                        for h in range(4):
                            self.tr(PTb[:, h * 128:(h + 1) * 128], ytk[:, h * 128:(h + 1) * 128], [bytk], [bPTf])
                        self.cp("dve", yb[:, kvh * 512:(kvh + 1) * 512], PTb[:, 0:512], [bPTf], [byb])
                        if kvh == 1 and wo is None:
                            self.dma("pool", self.YT1.ap()[t], yb[:], [byb], [self.bYT1[t]])
                    tails.append((i + 4, tail))
                    if kvh == 1 and wo is not None:
                        xn, bxn = XN.next()

                        def oproj(dg, yb=yb, byb=byb, t=t, xn=xn, bxn=bxn):
                            xr, bxr = xrs[t]
                            for kc in range(8):
                                self.mm(PTf[:], yb[:, kc * 128:(kc + 1) * 128], wo.w[:, kc, dg * 512:(dg + 1) * 512],
                                        kc == 0, kc == 7, [byb, wo.wb(kc, dg * 512)], [bPTf])
                            self.tt("dve", xn[:, dg * 512:(dg + 1) * 512], PTf[:], xr[:, dg * 512:(dg + 1) * 512], ALU.add,
                                    [bPTf, bxr], [bxn])
                            if dg == 1:
                                self.dma("pool", self.xres3.ap()[t], xn[:], [bxn], [self.bxres3[t]])
                        tails.append((i + 10, lambda f=oproj: f(0)))
                        tails.append((i + 16, lambda f=oproj: f(1)))

            n = len(its)
            for i in range(n + LA + 24):
                if i < n:
                    pend.append(front(*its[i]))
                if LA <= i < n + LA:
                    p_, bp = pend[i - LA]
                    back(i, *its[i - LA], p_, bp)
                while tails and tails[0][0] <= i:
                    tails.pop(0)[1]()
                if i % 16 == 8:
                    for L in loaders:
                        if L.todo:
                            L.step(1)
                            break
            assert not tails
            for L in loaders:
                L.finish()
            self.P.flush()

    def phase_G(self, pre=None):
        self.xres3 = self.dram("xres3", [NT, 128, D], F32)
        self.bxres3 = [Buf() for _ in range(NT)]
        with ExitStack() as st:
            if pre is not None:
                Wo, bWo = pre.w, pre.wb
            else:
                stg = self.sb(st, [128, 512], F32, n=6)
                Wo, bWo = self.load_w(st, self.w_out_odd, 0, D, D, stg)
            X = self.sb(st, [128, D], F32, n=2)
            YB = self.sb(st, [128, D], BF16, n=2)
            XN = self.sb(st, [128, D], F32, n=2)
            PO = self.ps(st, [128, 512], F32, n=4)
            for t in range(NT):
                xt, bx = X.next()
                yb, byb = YB.next()
                xn, bxn = XN.next()
                self.dma("sp", xt[:], self.xres2.ap()[t], [self.bxres2[t]], [bx])
                self.dma("sp", yb[:], self.YT1.ap()[t], [self.bYT1[t]], [byb])
                for dg in range(2):
                    po, bpo = PO.next()
                    for kc in range(8):
                        self.mm(po[:], yb[:, kc * 128:(kc + 1) * 128], Wo[:, kc, dg * 512:(dg + 1) * 512], kc == 0, kc == 7,
                                [byb, bWo(kc, dg * 512)], [bpo])
                    self.tt("dve", xn[:, dg * 512:(dg + 1) * 512], po[:], xt[:, dg * 512:(dg + 1) * 512], ALU.add,
                            [bpo, bx], [bxn])
                self.dma("pool", self.xres3.ap()[t], xn[:], [bxn], [self.bxres3[t]])
            self.P.flush()

    def finish(self):
        self.P.drain()
        self.P.flush()

    def build(self):
        self.setup()
        if getattr(self, "only", None) == "E":
            self.xres2 = self.nc.dram_tensor("xdbg", [NT, 128, D], F32, kind="ExternalInput")
            self.bxres2 = [Buf() for _ in range(NT)]
            self.phase_E()
            return self.finish()
        self.phase_A()
        if self.stage <= 1:
            return self.finish()
        self.phase_B1()
        if self.stage <= 2:
            return self.finish()
        with ExitStack() as pre0:
            stgP = self.sb(pre0, [128, 512], F32, n=6)
            L0 = WLoader(self, pre0, self.w_up, 0, D, 4096, stgP, q="pool", engs=("pool",))
            self.phase_B2(loader=L0)
            if self.stage <= 3:
                L0.finish()
                return self.finish()
            self.phase_B3(loader=L0)
            if self.stage <= 4:
                return self.finish()
            self.xres2 = self.dram("xres2", [NT, 128, D], F32)
            self.bxres2 = [Buf() for _ in range(NT)]
            self.phase_mlp(0, self.xres, self.bxres, lambda t: self.xres2.ap()[t], self.bxres2, pre_up=L0, stg=stgP)
            if self.stage <= 5:
                return self.finish()
        self.phase_E()
        if self.stage <= 6:
            return self.finish()
        with ExitStack() as pre1:
            stgP = self.sb(pre1, [128, 512], F32, n=6)
            Lo = WLoader(self, pre1, self.w_out_odd, 0, D, D, stgP, q="pool", engs=("pool",))
            L1 = WLoader(self, pre1, self.w_up, D, D, 4096, stgP, q="pool", engs=("pool",))
            self.phase_F(loaders=[Lo, L1], wo=Lo)
            if self.stage <= 8:
                return self.finish()
            self.by = [Buf() for _ in range(NT)]
            self.phase_mlp(1, self.xres3, self.bxres3, lambda t: self.y_out.ap()[t * 128:(t + 1) * 128, :], self.by,
                           pre_up=L1, stg=stgP)
            return self.finish()


def host_inputs(inputs):
    x = np.ascontiguousarray(inputs["x"], dtype=np.float32)
    S = x.shape[1]
    maps = []
    shared = {
        "norm_mix": inputs["norm_mix"], "norm_mlp": inputs["norm_mlp"],
        "w_in_even": inputs["w_in_even"][0], "w_out_even": inputs["w_out_even"][0],
        "ret_decay_logit": inputs["ret_decay_logit"].reshape(1, 8), "ret_norm": inputs["ret_norm"],
        "swa_q_norm": inputs["swa_q_norm"], "swa_k_norm": inputs["swa_k_norm"],
        "swa_sink": inputs["swa_sink"], "t5_table": inputs["t5_table"],
        "w_in_odd": inputs["w_in_odd"][0], "w_out_odd": inputs["w_out_odd"][0],
        "ax_q_norm": inputs["ax_q_norm"], "ax_k_norm": inputs["ax_k_norm"],
        "w_mlp_up": inputs["w_mlp_up"], "w_mlp_down": inputs["w_mlp_down"],
    }
    shared = {k: np.ascontiguousarray(v, dtype=np.float32) for k, v in shared.items()}
    for c in range(8):
        b, q = c // 4, c % 4
        t0 = q * TOK
        xs = np.zeros((TOK + 256, D), np.float32)
        lo, hi = max(t0 - 128, 0), min(t0 + TOK + 128, S)
        xs[lo - (t0 - 128):hi - (t0 - 128)] = x[b, lo:hi]
        meta = np.zeros((128, 8), np.float32)
        meta[:, 0] = t0
        meta[:, 1] = q
        meta[:, 2] = 1.0 if q > 0 else 0.0
        meta[:, 3] = 1.0 if q < 3 else 0.0
        m = dict(shared)
        m["x"] = xs
        m["meta"] = meta
        maps.append(m)
    return maps


def kernel(**inputs):
    mk = MK()
    mk.build()
    res = run_bass_kernel_spmd(mk.nc, host_inputs(inputs), core_ids=list(range(8)))
    out = np.empty((2, 4 * TOK, D), np.float32)
    for c in range(8):
        out[c // 4, (c % 4) * TOK:(c % 4 + 1) * TOK] = res.results[c]["y"]
    return out
```

```python
import math
from contextlib import ExitStack, nullcontext

import numpy as np
import concourse.bass as bass
import concourse.mybir as mybir
from concourse.bass_utils import run_bass_kernel_spmd

F32 = mybir.dt.float32
BF16 = mybir.dt.bfloat16
I32 = mybir.dt.int32
ALU = mybir.AluOpType
AF = mybir.ActivationFunctionType
AX = mybir.AxisListType

EPS = 1e-6
NT = 16
TOK = 2048
D = 1024
PI = math.pi
GROUPS = [[0, 1, 2, 3], [4, 5, 6, 7]]
NEG = -30000.0
EMBED_WAIT = True
ALT_Q = False


class Buf:
    __slots__ = ("name", "w", "r")

    def __init__(self, name=""):
        self.name = name
        self.w = None
        self.r = []


class _Eng:
    def __init__(self, name):
        self.name = name
        self.count = 0
        self.ops = []
        self.seen = {}


class Prog:
    ENGS = ("pe", "act", "dve", "pool", "sp")
    NRING = 12

    def __init__(self, nc):
        self.nc = nc
        self.eng = {n: _Eng(n) for n in self.ENGS}
        self.ring = {q: [0] * self.NRING for q in ("sp", "pool", "act")}
        self.ring_pos = {q: 0 for q in ("sp", "pool", "act")}
        self.cc_count = 0
        self.st = ExitStack()
        self.sems = {}
        for k in self.ENGS:
            self.sems[k] = self.st.enter_context(nc.semaphore("s_" + k))
        for q in ("sp", "pool", "act"):
            for i in range(self.NRING):
                k = "d_%s_%d" % (q, i)
                self.sems[k] = self.st.enter_context(nc.semaphore(k))
        for i in range(4):
            k = "cc%d" % i
            self.sems[k] = self.st.enter_context(nc.semaphore(k))

    def _record(self, eng, fn, reads, writes, ev_key, ev_val, inc, extra_need=()):
        E = self.eng[eng]
        waits = {}

        def need(ev):
            if ev is None:
                return
            k, v = ev
            if k == "pe" and eng == "pe":
                return
            if E.seen.get(k, 0) >= v:
                return
            if waits.get(k, 0) < v:
                waits[k] = v

        for b in reads:
            need(b.w)
        for b in writes:
            need(b.w)
            for ev in b.r:
                need(ev)
        for ev in extra_need:
            need(ev)
        for k, v in waits.items():
            E.seen[k] = v
        ev = (ev_key, ev_val)
        E.ops.append((tuple(waits.items()), fn, ev_key, inc))
        for b in reads:
            b.r.append(ev)
        for b in writes:
            b.w = ev
            b.r = []
        return ev

    def op(self, eng, fn, reads=(), writes=()):
        E = self.eng[eng]
        E.count += 1
        return self._record(eng, fn, reads, writes, eng, E.count, 1)

    def dma(self, q, out, in_, reads=(), writes=()):
        pos = self.ring_pos[q]
        self.ring_pos[q] = (pos + 1) % self.NRING
        key = "d_%s_%d" % (q, pos)
        prev = self.ring[q][pos]
        self.ring[q][pos] = prev + 16
        extra = [(key, prev)] if prev > 0 else []

        def fn(e):
            return e.dma_start(out=out, in_=in_)

        return self._record(q, fn, reads, writes, key, prev + 16, 16, extra)

    def cc(self, fn, reads=(), writes=()):
        key = "cc%d" % self.cc_count
        self.cc_count += 1
        return self._record("pool", fn, reads, writes, key, 1, 1)

    def drain(self, cc=True):
        extra = []
        for q in self.ring:
            for i, v in enumerate(self.ring[q]):
                if v > 0:
                    extra.append(("d_%s_%d" % (q, i), v))
        for i in range(self.cc_count if cc else 0):
            extra.append(("cc%d" % i, 1))
        E = self.eng["sp"]
        E.count += 1
        self._record("sp", lambda e: e.nop(), (), (), "sp", E.count, 1, extra)

    def flush(self):
        self.drain(cc=False)
        nc = self.nc
        sems = self.sems
        self.marks = getattr(self, "marks", [])
        self.marks.append({k: E.count for k, E in self.eng.items()})
        with nc.Block() as block:

            def run(E, e):
                for waits, fn, k, inc in E.ops:
                    if EMBED_WAIT and waits:
                        for wk, wv in waits[1:]:
                            e.wait_ge(sems[wk], wv)
                        ins = fn(e)
                        ins._wait_ge(sems[waits[0][0]], waits[0][1])
                        ins.then_inc(sems[k], inc)
                    else:
                        for wk, wv in waits:
                            e.wait_ge(sems[wk], wv)
                        fn(e).then_inc(sems[k], inc)
                E.ops = []

            @block.tensor
            def _(e):
                run(self.eng["pe"], e)

            @block.scalar
            def _(e):
                run(self.eng["act"], e)

            @block.vector
            def _(e):
                run(self.eng["dve"], e)

            @block.gpsimd
            def _(e):
                run(self.eng["pool"], e)

            @block.sync
            def _(e):
                run(self.eng["sp"], e)


class Pipe:
    def __init__(self):
        self.i = 0
        self.q = []

    def defer(self, k, fn):
        self.q.append((self.i + k, fn))

    def tick(self):
        self.i += 1
        due = [x for x in self.q if x[0] <= self.i]
        self.q = [x for x in self.q if x[0] > self.i]
        for _, fn in due:
            fn()

    def drain(self):
        while self.q:
            self.tick()


class WLoader:
    def __init__(self, mk, st, handle, row0, K, N, stg, q="sp", engs=("pool", "dve", "act")):
        self.mk, self.handle, self.row0, self.N, self.stg, self.q, self.engs = mk, handle, row0, N, stg, q, engs
        KC = K // 128
        self.w, _ = mk.sb(st, [128, KC, N], BF16)
        self.CH = stg.t[0].shape[1]
        assert N % self.CH == 0
        self.todo = [(kc, c0) for c0 in range(0, N, self.CH) for kc in range(KC)]
        self.bufs = {}
        self.pending = []
        self.i = 0

    def step(self, n=1, lag=4):
        mk, N, CH = self.mk, self.N, self.CH
        for _ in range(n):
            if self.todo:
                kc, c0 = self.todo.pop(0)
                t, b = self.stg.next()
                src = bass.AP(self.handle, (self.row0 + kc * 128) * N + c0, [[N, 128], [1, CH]])
                q = self.q
                if q == "sp" and ALT_Q and (len(self.todo) % 2):
                    q = "act"
                mk.dma(q, t[:], src, [], [b])
                self.pending.append((kc, c0, t, b))
            while self.pending and (len(self.pending) > lag or not self.todo):
                kc, c0, t, b = self.pending.pop(0)
                bw = Buf()
                mk.cp(self.engs[self.i % len(self.engs)], self.w[:, kc, c0:c0 + CH], t[:], [b], [bw])
                self.bufs[(kc, c0 // CH)] = bw
                self.i += 1
                if self.todo:
                    break

    def finish(self):
        while self.todo or self.pending:
            self.step(1)

    def wb(self, kc, col):
        key = (kc, col // self.CH)
        while key not in self.bufs:
            self.step(1)
        return self.bufs[key]


class Rot:
    def __init__(self, tensors):
        self.t = tensors
        self.b = [Buf() for _ in tensors]
        self.i = -1

    def next(self):
        self.i += 1
        k = self.i % len(self.t)
        return self.t[k], self.b[k]


class MK:
    def __init__(self, stage=99, dbg=()):
        self.nc = nc = bass.Bass("TRN2", target_bir_lowering=False)
        self.P = Prog(nc)
        self.stage = stage
        self.dbg = set(dbg)
        self.uid = 0
        self.gst = ExitStack()
        self.out_bufs = []

        def din(name, shape):
            return nc.dram_tensor(name, shape, F32, kind="ExternalInput")

        self.x_in = din("x", [TOK + 256, D])
        self.meta_in = din("meta", [128, 8])
        self.norm_mix = din("norm_mix", [2, D])
        self.norm_mlp = din("norm_mlp", [2, D])
        self.w_in_even = din("w_in_even", [D, 4608])
        self.w_out_even = din("w_out_even", [2048, D])
        self.ret_decay = din("ret_decay_logit", [1, 8])
        self.ret_norm = din("ret_norm", [1, D])
        self.swa_q_norm = din("swa_q_norm", [1, 128])
        self.swa_k_norm = din("swa_k_norm", [1, 128])
        self.swa_sink = din("swa_sink", [1, 8])
        self.t5_table = din("t5_table", [32, 8])
        self.w_in_odd = din("w_in_odd", [D, 1536])
        self.w_out_odd = din("w_out_odd", [D, D])
        self.ax_q_norm = din("ax_q_norm", [1, 128])
        self.ax_k_norm = din("ax_k_norm", [1, 128])
        self.w_up = din("w_mlp_up", [2, D, 4096])
        self.w_down = din("w_mlp_down", [2, 4096, D])
        self.y_out = nc.dram_tensor("y", [TOK, D], F32, kind="ExternalOutput")

    def _name(self, p):
        self.uid += 1
        return "%s%d" % (p, self.uid)

    def sb(self, st, shape, dt, n=1):
        ts = [st.enter_context(self.nc.sbuf_tensor(self._name("sb"), list(shape), dt)) for _ in range(n)]
        if n == 1:
            return ts[0], Buf()
        return Rot(ts)

    def ps(self, st, shape, dt, n=1):
        ts = [st.enter_context(self.nc.psum_tensor(self._name("ps"), list(shape), dt)) for _ in range(n)]
        if n == 1:
            return ts[0], Buf()
        return Rot(ts)

    def dram(self, name, shape, dt):
        kind = "ExternalOutput" if name in self.dbg else "Internal"
        return self.nc.dram_tensor(name, list(shape), dt, kind=kind)

    def mm(self, out, lhsT, rhs, start, stop, r, w):
        self.P.op("pe", lambda e: e.matmul(out, lhsT=lhsT, rhs=rhs, start=start, stop=stop), r, w)

    def tr(self, out, in_, r, w):
        ident = self.ident[:]
        self.P.op("pe", lambda e: e.transpose(out=out, in_=in_, identity=ident), list(r) + [self.bI], w)

    def act(self, out, in_, func, r, w, bias=None, scale=None, accum=None):
        kw = {}
        if bias is not None:
            kw["bias"] = bias
        if scale is not None:
            kw["scale"] = scale
        if accum is not None:
            kw["accum_out"] = accum
        self.P.op("act", lambda e: e.activation(out=out, in_=in_, func=func, **kw), r, w)

    def tt(self, eng, out, in0, in1, op, r, w):
        self.P.op(eng, lambda e: e.tensor_tensor(out=out, in0=in0, in1=in1, op=op), r, w)

    def ts(self, eng, out, in0, s1, s2, op0, op1, r, w):
        if op1 is None:
            self.P.op(eng, lambda e: e.tensor_scalar(out=out, in0=in0, scalar1=s1, scalar2=None, op0=op0), r, w)
        else:
            self.P.op(eng, lambda e: e.tensor_scalar(out=out, in0=in0, scalar1=s1, scalar2=s2, op0=op0, op1=op1), r, w)

    def stt(self, eng, out, in0, scalar, in1, op0, op1, r, w):
        self.P.op(eng, lambda e: e.scalar_tensor_tensor(out=out, in0=in0, scalar=scalar, in1=in1, op0=op0, op1=op1), r, w)

    def cp(self, eng, out, in_, r, w):
        if eng == "act":
            self.P.op(eng, lambda e: e.copy(out=out, in_=in_), r, w)
        else:
            self.P.op(eng, lambda e: e.tensor_copy(out=out, in_=in_), r, w)

    def memset(self, eng, ap, val, w):
        self.P.op(eng, lambda e: e.memset(ap, val), [], w)

    def iota(self, out, pattern, base, cm, w):
        self.P.op("pool", lambda e: e.iota(out, pattern=pattern, base=base, channel_multiplier=cm), [], w)

    def recip(self, out, in_, r, w):
        self.P.op("dve", lambda e: e.reciprocal(out=out, in_=in_), r, w)

    def rsum(self, out, in_, r, w):
        self.P.op("dve", lambda e: e.reduce_sum(out=out, in_=in_, axis=AX.X), r, w)

    def dma(self, q, out, in_, r, w):
        self.P.dma(q, out, in_, r, w)

    def bcast_row(self, handle, off, n):
        return bass.AP(handle, off, [[0, 128], [1, n]])

    def rstd(self, st, b, n, inv_count):
        self.ts("dve", st[:, 4:4 + n], st[:, 0:n], inv_count, EPS, ALU.mult, ALU.add, [b], [b])
        self.act(st[:, 8:8 + n], st[:, 4:4 + n], AF.Ln, [b], [b])
        self.act(st[:, 12:12 + n], st[:, 8:8 + n], AF.Exp, [b], [b], scale=-0.5)

    def load_w(self, st, handle, row0, K, N, stg):
        L = WLoader(self, st, handle, row0, K, N, stg)
        self.last_loader = L
        return L.w, L.wb

    def norm_T(self, xt, bx, gain, bg, junk, bj, st, bst, hb, bhb, pT, bpT, hT, bhT):
        self.norm_part(xt, bx, gain, bg, junk, bj, st, bst, hb, bhb)
        self.tr_part(hb, bhb, pT, bpT, hT, bhT)

    def norm_part(self, xt, bx, gain, bg, junk, bj, st, bst, hb, bhb):
        self.act(junk[:], xt[:], AF.Square, [bx], [bj, bst], accum=st[:, 0:1])
        self.rstd(st, bst, 1, 1.0 / D)
        self.stt("dve", hb[:], xt[:], st[:, 12:13], gain[:], ALU.mult, ALU.mult, [bx, bst, bg], [bhb])

    def tr_part(self, hb, bhb, pT, bpT, hT, bhT):
        for kc in range(8):
            self.tr(pT[:, kc * 128:(kc + 1) * 128], hb[:, kc * 128:(kc + 1) * 128], [bhb], [bpT])
        self.cp("act", hT[:], pT[:], [bpT], [bhT])

    def setup(self):
        st = self.gst
        self.ident, self.bI = self.sb(st, [128, 128], BF16)
        self.ones, self.bOnes = self.sb(st, [128, 128], BF16)
        self.meta, self.bMeta = self.sb(st, [128, 8], F32)
        self.lg, self.bLg = self.sb(st, [128, 8], F32)
        self.pcol, self.bPcol = self.sb(st, [128, 1], F32)
        self.negpi, self.bNegpi = self.sb(st, [128, 1], F32)
        self.memset("pool", self.negpi[:], -PI, [self.bNegpi])
        ident = self.ident[:]
        self.memset("pool", ident, 1.0, [self.bI])
        self.P.op("pool", lambda e: e.affine_select(out=ident, in_=ident, pattern=[[-1, 128]],
                                                    compare_op=ALU.is_equal, fill=0.0, base=0,
                                                    channel_multiplier=1), [self.bI], [self.bI])
        self.memset("pool", self.ones[:], 1.0, [self.bOnes])
        self.dma("sp", self.meta[:], self.meta_in.ap(), [], [self.bMeta])
        with nullcontext(st) as lst:
            tmp, bt = self.sb(lst, [128, 8], F32)
            pci, bpci = self.sb(lst, [128, 1], I32)
            self.dma("sp", tmp[:], self.bcast_row(self.ret_decay, 0, 8), [], [bt])
            self.act(tmp[:], tmp[:], AF.Exp, [bt], [bt], scale=-1.0)
            self.act(tmp[:], tmp[:], AF.Ln, [bt], [bt], bias=1.0)
            self.ts("dve", self.lg[:], tmp[:], -1.0, None, ALU.mult, None, [bt], [self.bLg])
            self.iota(pci[:], [[0, 1]], 0, 1, [bpci])
            self.cp("dve", self.pcol[:], pci[:], [bpci], [self.bPcol])

    def sincos(self, st, ang, bang, shape, cos_out, sin_out, bouts):
        C1 = 6.28125
        C2_ = 2 * PI - C1
        sl = tuple(slice(None) for _ in shape)
        t, bt = self.sb(st, shape, F32)
        ki, bki = self.sb(st, shape, I32)
        kf, bkf = self.sb(st, shape, F32)
        r, br = self.sb(st, shape, F32)
        m, bm = self.sb(st, shape, F32)
        self.ts("dve", t[sl], ang, 1.0 / (2 * PI), None, ALU.mult, None, [bang], [bt])
        self.cp("dve", ki[sl], t[sl], [bt], [bki])
        self.cp("dve", kf[sl], ki[sl], [bki], [bkf])
        self.stt("dve", r[sl], kf[sl], -C1, ang, ALU.mult, ALU.add, [bkf, bang], [br])
        self.stt("dve", r[sl], kf[sl], -C2_, r[sl], ALU.mult, ALU.add, [bkf, br], [br])

        def wrap(x, bx):
            self.ts("dve", m[sl], x[sl], PI, None, ALU.is_gt, None, [bx], [bm])
            self.stt("dve", x[sl], m[sl], -2 * PI, x[sl], ALU.mult, ALU.add, [bm, bx], [bx])
            self.ts("dve", m[sl], x[sl], -PI, None, ALU.is_lt, None, [bx], [bm])
            self.stt("dve", x[sl], m[sl], 2 * PI, x[sl], ALU.mult, ALU.add, [bm, bx], [bx])

        wrap(r, br)
        self.act(sin_out, r[sl], AF.Sin, [br], bouts)
        self.ts("dve", t[sl], r[sl], 0.5 * PI, None, ALU.add, None, [br], [bt])
        wrap(t, bt)
        self.act(cos_out, t[sl], AF.Sin, [bt], bouts)

    def phase_A(self):
        nc = self.nc
        self.QT_ret = self.dram("QT_ret", [NT, 128, 512], BF16)
        self.KT_ret = self.dram("KT_ret", [NT, 128, 512], BF16)
        self.KWF = self.dram("KWF", [NT, 128, 512], BF16)
        self.KWB = self.dram("KWB", [NT, 128, 512], BF16)
        self.V_ret = self.dram("V_ret", [NT, 128, 1024], BF16)
        self.G_ret = self.dram("G_ret", [NT, 128, 1024], BF16)
        self.QT_swa = self.dram("QT_swa", [NT, 128, 1024], BF16)
        self.KT_swa = self.dram("KT_swa", [NT + 2, 128, 256], BF16)
        self.V_swa = self.dram("V_swa", [NT + 2, 128, 256], BF16)
        self.bQT_ret = [Buf() for _ in range(NT)]
        self.bKT_ret = [Buf() for _ in range(NT)]
        self.bKWF = [Buf() for _ in range(NT)]
        self.bKWB = [Buf() for _ in range(NT)]
        self.bV_ret = [Buf() for _ in range(NT)]
        self.bG_ret = [Buf() for _ in range(NT)]
        self.bQT_swa = [Buf() for _ in range(NT)]
        self.bKT_swa = [Buf() for _ in range(NT + 2)]
        self.bV_swa = [Buf() for _ in range(NT + 2)]
        with ExitStack() as st:
            stg = self.sb(st, [128, 512], F32, n=6)
            W, bW = self.load_w(st, self.w_in_even, 0, D, 4608, stg)
            gain, bg = self.sb(st, [128, D], F32)
            gng, bgng = self.sb(st, [128, D], F32)
            gq, bgq = self.sb(st, [128, 128], F32)
            gk, bgk = self.sb(st, [128, 128], F32)
            self.dma("sp", gain[:], self.bcast_row(self.norm_mix, 0, D), [], [bg])
            self.dma("sp", gng[:], self.bcast_row(self.ret_norm, 0, D), [], [bgng])
            self.dma("sp", gq[:], self.bcast_row(self.swa_q_norm, 0, 128), [], [bgq])
            self.dma("sp", gk[:], self.bcast_row(self.swa_k_norm, 0, 128), [], [bgk])
            self.ts("dve", gq[:], gq[:], 128.0 ** -0.5, None, ALU.mult, None, [bgq], [bgq])
            gcol, bgcol = self.sb(st, [128, 2], F32)
            self.dma("sp", gcol[:, 0:1], bass.AP(self.swa_q_norm, 0, [[1, 128], [1, 1]]), [], [bgcol])
            self.dma("sp", gcol[:, 1:2], bass.AP(self.swa_k_norm, 0, [[1, 128], [1, 1]]), [bgcol], [bgcol])
            self.ts("dve", gcol[:, 0:1], gcol[:, 0:1], 128.0 ** -0.5, None, ALU.mult, None, [bgcol], [bgcol])
            C2, bC2 = self.sb(st, [128, NT, 128], F32)
            S2, bS2 = self.sb(st, [128, NT, 128], F32)
            kw, bkw = self.sb(st, [128, 8], F32)
            with nullcontext(st) as tst:
                invf, binv = self.sb(tst, [128, 64], F32)
                posi, bposi = self.sb(tst, [128, NT], I32)
                posf, bposf = self.sb(tst, [128, NT], F32)
                ang, bang = self.sb(tst, [128, NT, 64], F32)
                for j in range(64):
                    self.memset("pool", invf[:, j:j + 1], float(np.float32(10000.0) ** np.float32(-2.0 * j / 128.0)), [binv])
                self.iota(posi[:], [[128, NT]], 0, 1, [bposi])
                self.cp("dve", posf[:], posi[:], [bposi], [bposf])
                self.ts("dve", posf[:], posf[:], self.meta[:, 0:1], None, ALU.add, None, [bposf, self.bMeta], [bposf])
                self.tt("dve", ang[:], invf[:].unsqueeze(1).broadcast_to([128, NT, 64]),
                        posf[:].unsqueeze(2).broadcast_to([128, NT, 64]), ALU.mult, [binv, bposf], [bang])
                C2v = C2[:].rearrange("p t (two d) -> p t two d", two=2)
                S2v = S2[:].rearrange("p t (two d) -> p t two d", two=2)
                self.sincos(tst, ang[:], bang, [128, NT, 64], C2v[:, :, 0, :], S2v[:, :, 1, :], [bC2, bS2])
                self.cp("dve", C2v[:, :, 1, :], C2v[:, :, 0, :], [bC2], [bC2])
                self.ts("dve", S2v[:, :, 0, :], S2v[:, :, 1, :], -1.0, None, ALU.mult, None, [bS2], [bS2])
                t127, bt127 = self.sb(tst, [128, 1], F32)
                self.ts("dve", t127[:], self.pcol[:], -1.0, 127.0, ALU.mult, ALU.add, [self.bPcol], [bt127])
                self.ts("dve", kw[:, 0:4], self.lg[:, 0:4], t127[:, 0:1], None, ALU.mult, None, [self.bLg, bt127], [bkw])
                self.ts("dve", kw[:, 4:8], self.lg[:, 4:8], self.pcol[:, 0:1], None, ALU.mult, None, [self.bLg, self.bPcol, bkw], [bkw])
                self.act(kw[:], kw[:], AF.Exp, [bkw], [bkw])
                pass
            X = self.sb(st, [128, D], F32, n=2)
            junk, bj = self.sb(st, [128, D], BF16)
            ST = self.sb(st, [128, 16], F32, n=2)
            HB = self.sb(st, [128, D], BF16, n=2)
            HT = self.sb(st, [128, D], BF16, n=2)
            pT, bpT = self.ps(st, [128, D], BF16)
            PS = self.ps(st, [128, 512], F32, n=5)
            PQ = self.ps(st, [128, 512], BF16, n=2)
            TA = self.sb(st, [128, 512], F32, n=2)
            TB = self.sb(st, [128, 512], F32, n=2)
            QR = self.sb(st, [128, 512], BF16, n=7)
            QTt = self.sb(st, [128, 512], BF16, n=2)
            KW = self.sb(st, [128, 512], BF16, n=4)
            VB = self.sb(st, [128, D], BF16, n=2)
            GB = self.sb(st, [128, D], BF16, n=2)
            QS = self.sb(st, [128, D], BF16, n=2)
            ST4 = self.sb(st, [128, 16], F32, n=3)
            KS = self.sb(st, [128, 256], BF16, n=2)
            VS = self.sb(st, [128, 256], BF16, n=2)
            TA = self.sb(st, [128, 512], F32, n=3)
            TB = self.sb(st, [128, 512], F32, n=3)

            def rope(ps, bps, t, dst, bdst):
                ta, bta = TA.next()
                tb, btb = TB.next()
                qv = ps[:].rearrange("p (h two d) -> p h two d", h=4, two=2)
                Cb = C2[:, t, :].rearrange("p (two d) -> p two d", two=2).unsqueeze(1).broadcast_to([128, 4, 2, 64])
                Sb = S2[:, t, :].rearrange("p (two d) -> p two d", two=2).unsqueeze(1).broadcast_to([128, 4, 2, 64])
                tav = ta[:].rearrange("p (h two d) -> p h two d", h=4, two=2)
                tbv = tb[:].rearrange("p (h two d) -> p h two d", h=4, two=2)
                self.tt("dve", tav, qv, Cb, ALU.mult, [bps, bC2], [bta])
                self.tt("dve", tbv, qv[:, :, ::-1, :], Sb, ALU.mult, [bps, bS2], [btb])
                self.tt("pool", dst[:], ta[:], tb[:], ALU.add, [bta, btb], [bdst])

            def transposes(src, bsrc, n, dst_ap, bdst, scale=None):
                pq, bpq = PQ.next()
                for h in range(n):
                    self.tr(pq[:, h * 128:(h + 1) * 128], src[:, h * 128:(h + 1) * 128], [bsrc], [bpq])
                if scale is None:
                    self.cp("act", dst_ap, pq[:, 0:n * 128], [bpq], [bdst])
                else:
                    self.act(dst_ap, pq[:, 0:n * 128], AF.Copy, [bpq, bgcol], [bdst], scale=scale)

            def headnorm(ps_ap, bps, n, g, bgn, dst_ap, bdst):
                st4, bst4 = ST4.next()
                ta, bta = TA.next()
                tb, btb = TB.next()
                self.act(ta[:, 0:n * 128], ps_ap, AF.Square, [bps], [bta])
                self.rsum(st4[:, 0:n], ta[:, 0:n * 128].rearrange("p (h d) -> p h d", h=n), [bta], [bst4])
                self.rstd(st4, bst4, n, 1.0 / 128)
                self.tt("dve", dst_ap.rearrange("p (h d) -> p h d", h=n),
                        ps_ap.rearrange("p (h d) -> p h d", h=n),
                        st4[:, 12:12 + n].unsqueeze(2).broadcast_to([128, n, 128]), ALU.mult, [bps, bst4], [bdst])

            pipe = Pipe()
            tiles = list(range(NT)) + [NT, NT + 1]

            def start_norm(tt_):
                own = tt_ < NT
                r0 = 128 + tt_ * 128 if own else (0 if tt_ == NT else TOK + 128)
                xt, bx = X.next()
                stt_, bst = ST.next()
                hb, bhb = HB.next()
                self.dma("sp", xt[:], self.x_in.ap()[r0:r0 + 128, :], [], [bx])
                self.norm_part(xt, bx, gain, bg, junk, bj, stt_, bst, hb, bhb)
                return hb, bhb

            def start_tr(hb, bhb):
                hT, bhT = HT.next()
                self.tr_part(hb, bhb, pT, bpT, hT, bhT)
                return hT, bhT

            cur = start_tr(*start_norm(tiles[0]))
            for idx, tt_ in enumerate(tiles):
                own = tt_ < NT
                sidx = tt_ + 1 if own else (0 if tt_ == NT else NT + 1)
                hT, bhT = cur
                groups = list(range(9)) if own else [8]
                vb = gb = qs = None
                for gi, g in enumerate(groups):
                    ps, bps = PS.next()
                    for kc in range(8):
                        self.mm(ps[:], hT[:, kc * 128:(kc + 1) * 128], W[:, kc, g * 512:(g + 1) * 512],
                                kc == 0, kc == 7, [bhT, bW(kc, g * 512)], [bps])
                    if g == 0 or g == 1:
                        qr, bqr = QR.next()
                        rope(ps, bps, tt_, qr, bqr)

                        def stage2(g=g, qr=qr, bqr=bqr, tt_=tt_):
                            qT, bqT = QTt.next()
                            transposes(qr, bqr, 4, qT[:], bqT)
                            if g == 0:
                                self.dma("pool", self.QT_ret.ap()[tt_], qT[:], [bqT], [self.bQT_ret[tt_]])
                            else:
                                self.dma("pool", self.KT_ret.ap()[tt_], qT[:], [bqT], [self.bKT_ret[tt_]])
                        pipe.defer(3, stage2)
                        if g == 1:
                            for d_, (dst, bdst) in enumerate(((self.KWF, self.bKWF), (self.KWB, self.bKWB))):
                                kwt, bkwt = KW.next()
                                self.tt("dve", kwt[:].rearrange("p (h d) -> p h d", h=4),
                                        qr[:].rearrange("p (h d) -> p h d", h=4),
                                        kw[:, d_ * 4:d_ * 4 + 4].unsqueeze(2).broadcast_to([128, 4, 128]),
                                        ALU.mult, [bqr, bkw], [bkwt])
                                self.dma("pool", dst.ap()[tt_], kwt[:], [bkwt], [bdst[tt_]])
                    elif g in (2, 3):
                        if g == 2:
                            vb, bvb = VB.next()
                        self.cp("act", vb[:, (g - 2) * 512:(g - 1) * 512], ps[:], [bps], [bvb])
                        if g == 3:
                            self.dma("pool", self.V_ret.ap()[tt_], vb[:], [bvb], [self.bV_ret[tt_]])
                    elif g in (4, 5):
                        if g == 4:
                            gb, bgb = GB.next()
                        ta, bta = TA.next()
                        self.act(ta[:], ps[:], AF.Silu, [bps], [bta])
                        self.tt("dve", gb[:, (g - 4) * 512:(g - 3) * 512], ta[:], gng[:, (g - 4) * 512:(g - 3) * 512],
                                ALU.mult, [bta, bgng], [bgb])
                        if g == 5:
                            self.dma("pool", self.G_ret.ap()[tt_], gb[:], [bgb], [self.bG_ret[tt_]])
                    elif g in (6, 7):
                        if g == 6:
                            qs, bqs = QS.next()
                        qr, bqr = QR.next()
                        headnorm(ps[:], bps, 4, gq, bgq, qr[:], bqr)

                        def stage2(g=g, qr=qr, bqr=bqr, qs=qs, bqs=bqs, tt_=tt_):
                            transposes(qr, bqr, 4, qs[:, (g - 6) * 512:(g - 5) * 512], bqs, scale=gcol[:, 0:1])
                            if g == 7:
                                self.dma("pool", self.QT_swa.ap()[tt_], qs[:], [bqs], [self.bQT_swa[tt_]])
                        pipe.defer(4, stage2)
                    else:
                        qr, bqr = QR.next()
                        headnorm(ps[:, 0:256], bps, 2, gk, bgk, qr[:, 0:256], bqr)

                        def stage2(qr=qr, bqr=bqr, sidx=sidx):
                            ks, bks = KS.next()
                            transposes(qr, bqr, 2, ks[:], bks, scale=gcol[:, 1:2])
                            self.dma("pool", self.KT_swa.ap()[sidx], ks[:], [bks], [self.bKT_swa[sidx]])
                        pipe.defer(4, stage2)
                        vs, bvs = VS.next()
                        self.cp("act", vs[:], ps[:, 256:512], [bps], [bvs])
                        self.dma("pool", self.V_swa.ap()[sidx], vs[:], [bvs], [self.bV_swa[sidx]])
                    if gi == 0 and idx + 1 < len(tiles):
                        nhb = start_norm(tiles[idx + 1])
                    if gi == min(5, len(groups) - 1) and idx + 1 < len(tiles):
                        nxt = start_tr(*nhb)
                    pipe.tick()
                cur = nxt
            pipe.drain()
            self.P.flush()

    def load_tiles_split(self, st, handle, n, cols, bufs, g=4, queues=("sp", "act"), order=None):
        t, _ = self.sb(st, [128, n, cols], BF16)
        tb = [None] * n
        groups = list(range(0, n, g))
        if order == "desc":
            groups = groups[::-1]
        for i, t0 in enumerate(groups):
            b = Buf()
            self.dma(queues[i % len(queues)], t[:, t0:t0 + g, :], handle.ap()[t0:t0 + g].rearrange("t p c -> p t c"),
                     bufs[t0:t0 + g], [b])
            for k in range(t0, min(t0 + g, n)):
                tb[k] = b
        return t, tb

    def load_tiles(self, st, handle, n, cols, bufs, q="sp"):
        t, b = self.sb(st, [128, n, cols], BF16)
        self.dma(q, t[:], handle.ap().rearrange("t p c -> p t c"), bufs, [b])
        return t, b

    def state_step(self, psU, bpsU, kwf, bKW, vf, bV, state, bstate, gC, k0):
        for h in range(4):
            pu, bpu = psU[h // 2], bpsU[h // 2]
            self.mm(pu[:, (h % 2) * 256:(h % 2 + 1) * 256], kwf(h), vf(h), True, True, [bKW, bV], [bpu])
        for h in range(4):
            pu, bpu = psU[h // 2], bpsU[h // 2]
            self.stt("dve", state[:, k0 + h, :], state[:, k0 + h, :], gC[:, k0 + h:k0 + h + 1],
                     pu[:, (h % 2) * 256:(h % 2 + 1) * 256], ALU.mult, ALU.add,
                     [bstate[k0 + h], bpu, self.bGC], [bstate[k0 + h]])

    def phase_B1(self):
        self.st_src = self.dram("st_src", [8 * 128, 256], F32)
        self.st_all = self.dram("st_all", [4 * 8 * 128, 256], F32)
        self.b_st_src = Buf()
        self.b_st_all = Buf()
        self.gC, self.bGC = self.sb(self.gst, [128, 8], F32)
        self.act(self.gC[:], self.lg[:], AF.Exp, [self.bLg], [self.bGC], scale=128.0)
        with ExitStack() as st:
            KWF, bKWF = self.load_tiles_split(st, self.KWF, NT, 512, self.bKWF)
            KWB, bKWB = self.load_tiles_split(st, self.KWB, NT, 512, self.bKWB, order="desc")
            V, bV = self.load_tiles_split(st, self.V_ret, NT, 1024, self.bV_ret, g=2)
            state, _ = self.sb(st, [128, 8, 256], F32)
            bstate = [Buf() for _ in range(8)]
            self.memset("pool", state[:], 0.0, bstate)
            psU = [self.ps(st, [128, 512], F32) for _ in range(2)]
            pU, bpU = [p[0] for p in psU], [p[1] for p in psU]
            for c in range(NT):
                self.state_step(pU, bpU, lambda h, c=c: KWF[:, c, h * 128:(h + 1) * 128], bKWF[c],
                                lambda h, c=c: V[:, c, h * 256:(h + 1) * 256], bV[c], state, bstate, self.gC, 0)
            for c in reversed(range(NT)):
                self.state_step(pU, bpU, lambda h, c=c: KWB[:, c, h * 128:(h + 1) * 128], bKWB[c],
                                lambda h, c=c: V[:, c, h * 256:(h + 1) * 256], bV[c], state, bstate, self.gC, 4)
            self.dma("sp", self.st_src.ap().rearrange("(k d) e -> d k e", d=128), state[:], bstate, [self.b_st_src])
            src, dst = self.st_src.ap(), self.st_all.ap()
            self.P.cc(lambda e: e.collective_compute("AllGather", ALU.bypass, replica_groups=GROUPS,
                                                     ins=[src], outs=[dst]), [self.b_st_src], [self.b_st_all])
            self.P.flush()

    def phase_B2(self, loader=None):
        self.YbT = self.dram("YbT", [NT, 128, 1024], BF16)
        self.bYbT = [Buf() for _ in range(NT)]
        TBd = self.dram("TBd", [8, 768], F32)
        bTBd = Buf()
        with ExitStack() as st:
            BT, bBT = self.sb(st, [128, 3 * 8 * 128], F32)
            esink, bes = self.sb(st, [128, 8], F32)
            offs, boffs = self.sb(st, [128, 2], F32)
            self.dma("sp", esink[:], self.bcast_row(self.swa_sink, 0, 8), [], [bes])
            self.act(esink[:], esink[:], AF.Exp, [bes], [bes])
            self.ts("dve", offs[:], self.meta[:, 2:4], -1.0, -NEG, ALU.add, ALU.mult, [self.bMeta], [boffs])
            with nullcontext(st) as tst:
                NM = 3 * 255
                reli, b0 = self.sb(tst, [32, NM], I32)
                rel, b1 = self.sb(tst, [32, NM], F32)
                n, b2 = self.sb(tst, [32, NM], F32)
                a, b3 = self.sb(tst, [32, NM], F32)
                tmp, b4 = self.sb(tst, [32, NM], F32)
                oh, b5 = self.sb(tst, [32, NM], F32)
                t5, b6 = self.sb(tst, [32, 8], F32)
                tb, b7 = self.sb(tst, [8, 768], F32)
                J, bJ = self.sb(tst, [128, 128], F32)
                H, bH = self.sb(tst, [128, 3 * 8 * 128], F32)
                pb, bpb = self.ps(tst, [128, 512], F32)
                self.dma("sp", t5[:], self.t5_table.ap(), [], [b6])
                self.iota(reli[:].rearrange("p (a m) -> p a m", a=3), [[128, 3], [-1, 255]], -1, 0, [b0])
                self.cp("dve", rel[:], reli[:], [b0], [b1])
                self.stt("dve", n[:], rel[:], -1.0, rel[:], ALU.mult, ALU.max, [b1], [b2])
                self.ts("dve", a[:], n[:], 8.0, None, ALU.min, None, [b2], [b3])
                for thr in (12, 16, 23, 32, 46, 64, 91):
                    self.stt("dve", a[:], n[:], float(thr), a[:], ALU.is_ge, ALU.add, [b2, b3], [b3])
                self.ts("dve", tmp[:], rel[:], 0.0, 16.0, ALU.is_gt, ALU.mult, [b1], [b4])
                self.tt("dve", a[:], a[:], tmp[:], ALU.add, [b3, b4], [b3])
                self.ts("dve", oh[:], a[:], self.pcol[0:32, 0:1], None, ALU.is_equal, None, [b3, self.bPcol], [b5])
                self.ts("dve", n[:], n[:], 128.0, None, ALU.is_le, None, [b2], [b2])
                self.ts("dve", tmp[:], n[:], -1.0, -NEG, ALU.add, ALU.mult, [b2, b4], [b4])
                for c0, c1 in ((0, 512), (512, NM)):
                    self.mm(pb[0:8, 0:c1 - c0], t5[:, :], oh[:, c0:c1], True, True, [b6, b5], [bpb])
                    self.tt("dve", tb[:, c0:c1], pb[0:8, 0:c1 - c0], n[0:8, c0:c1], ALU.mult, [bpb, b2], [b7])
                    self.tt("dve", tb[:, c0:c1], tb[:, c0:c1], tmp[0:8, c0:c1], ALU.add, [b7, b4], [b7])
                self.dma("sp", TBd.ap()[:, 0:NM], tb[:, 0:NM], [b7], [bTBd])
                for jt in range(3):
                    self.dma("sp", H[:, jt * 1024:(jt + 1) * 1024].rearrange("p (h q) -> p h q", h=8),
                             bass.AP(TBd, jt * 255, [[1, 128], [768, 8], [1, 128]]), [bTBd], [bH])
                self.memset("pool", J[:], 1.0, [bJ])
                Jap = J[:]
                self.P.op("pool", lambda e: e.affine_select(out=Jap, in_=Jap, pattern=[[1, 128]],
                                                            compare_op=ALU.is_equal, fill=0.0, base=-127,
                                                            channel_multiplier=1), [bJ], [bJ])
                for i in range(6):
                    self.mm(pb[:], J[:], H[:, i * 512:(i + 1) * 512], True, True, [bJ, bH], [bpb])
                    self.cp("dve", BT[:, i * 512:(i + 1) * 512], pb[:], [bpb], [bBT])
                pass
            BTb, bBTb = self.sb(st, [128, 3 * 8 * 128], BF16)
            self.cp("dve", BTb[:], BT[:], [bBT], [bBTb])
            BTv = BTb[:].rearrange("p (a h q) -> p a (h q)", a=3, h=8)
            KT, bKT = self.load_tiles(st, self.KT_swa, NT + 2, 256, self.bKT_swa)
            V, bV = self.load_tiles(st, self.V_swa, NT + 2, 256, self.bV_swa)
            QT = self.sb(st, [128, 1024], BF16, n=2)
            PSS = self.ps(st, [128, 512], F32, n=3)
            PD = self.ps(st, [128, 512], F32, n=2)
            PO = self.ps(st, [128, 512], F32, n=2)
            Pb = self.sb(st, [128, 512], BF16, n=5)
            DN = self.sb(st, [128, 512], F32, n=2)
            YB = self.sb(st, [128, 1024], BF16, n=2)
            LA = 2
            onesr, bonesr = self.sb(st, [1, 128], F32)
            esr, besr = self.sb(st, [1, 1024], F32)
            self.memset("pool", onesr[:], 1.0, [bonesr])
            self.cp("dve", esr[:].rearrange("p (h q) -> p h q", h=8), esink[0:1, :].unsqueeze(2).broadcast_to([1, 8, 128]),
                    [bes], [besr])
            its = [(t, kvh, jt) for t in range(NT) for kvh in range(2) for jt in range(3)]
            qts, ybs, units = {}, {}, {}
            pend = []
            tails = []

            def front(t, kvh, jt):
                if kvh == 0 and jt == 0:
                    qt, bqt = QT.next()
                    self.dma("sp", qt[:], self.QT_swa.ap()[t], [self.bQT_swa[t]], [bqt])
                    qts[t] = (qt, bqt)
                    ybs[t] = YB.next()
                qt, bqt = qts[t]
                sT, bsT = PSS.next()
                self.mm(sT[:], self.ident[:], BTv[:, jt, kvh * 512:(kvh + 1) * 512], True, False, [self.bI, bBTb], [bsT])
                self.mm(sT[:], KT[:, t + jt, kvh * 128:(kvh + 1) * 128], qt[:, kvh * 512:(kvh + 1) * 512],
                        False, True, [bKT, bqt], [bsT])
                p_, bp = Pb.next()
                if t == 0 and jt == 0:
                    self.act(p_[:], sT[:], AF.Exp, [bsT, boffs], [bp], bias=offs[:, 0:1])
                elif t == NT - 1 and jt == 2:
                    self.act(p_[:], sT[:], AF.Exp, [bsT, boffs], [bp], bias=offs[:, 1:2])
                else:
                    self.act(p_[:], sT[:], AF.Exp, [bsT], [bp])
                return p_, bp

            def back(i, t, kvh, jt, p_, bp):
                if jt == 0:
                    units[(t, kvh)] = (PD.next(), PO.next())
                (den, bden), (o, bo) = units[(t, kvh)]
                if jt == 0:
                    self.mm(den[:], onesr[:], esr[:, kvh * 512:(kvh + 1) * 512], True, False, [bonesr, besr], [bden])
                self.mm(den[:], self.ones[:], p_[:], False, jt == 2, [self.bOnes, bp], [bden])
                self.mm(o[:], V[:, t + jt, kvh * 128:(kvh + 1) * 128], p_[:], jt == 0, jt == 2, [bV, bp], [bo])
                if jt == 2:
                    dn, bdn = DN.next()
                    yb, byb = ybs[t]
                    self.recip(dn[:], den[:], [bden], [bdn])
                    self.tt("dve", yb[:, kvh * 512:(kvh + 1) * 512], o[:], dn[:], ALU.mult, [bo, bdn], [byb])
                    if kvh == 1:
                        self.dma("pool", self.YbT.ap()[t], yb[:], [byb], [self.bYbT[t]])

            n = len(its)
            for i in range(n + LA):
                if i < n:
                    pend.append(front(*its[i]))
                if i >= LA:
                    p_, bp = pend[i - LA]
                    back(i, *its[i - LA], p_, bp)
                if loader is not None and i % 2 == 0:
                    loader.step(1)
            self.P.flush()

    def phase_B3(self, loader=None):
        self.SBd = self.dram("SBd", [NT, 128, 1024], BF16)
        self.bSBd = [Buf() for _ in range(NT)]
        self.xres = self.dram("xres", [NT, 128, D], F32)
        self.bxres = [Buf() for _ in range(NT)]
        SC = 128.0 ** -0.5
        with ExitStack() as st:
            S_in, _ = self.sb(st, [128, 8, 256], F32)
            bS = [Buf() for _ in range(8)]
            with ExitStack() as tst:
                V, bV = self.load_tiles_split(tst, self.V_ret, NT, 1024, self.bV_ret, g=2, order="desc")
                SBT = self.sb(tst, [128, 1024], BF16, n=2)
                ri, b0 = self.sb(tst, [128, 4], I32)
                rv, b1 = self.sb(tst, [128, 4], F32)
                ev, b2 = self.sb(tst, [128, 8], F32)
                msk, b3 = self.sb(tst, [128, 8], F32)
                coef, b4 = self.sb(tst, [128, 8, 4], F32)
                ip, b5 = self.sb(tst, [128, 2], F32)
                self.iota(ri[:], [[1, 4]], 0, 0, [b0])
                self.cp("dve", rv[:], ri[:], [b0], [b1])
                self.ts("dve", ip[:, 0:1], self.meta[:, 1:2], -1.0, None, ALU.add, None, [self.bMeta], [b5])
                self.ts("dve", ip[:, 1:2], self.meta[:, 1:2], 1.0, None, ALU.add, None, [self.bMeta, b5], [b5])
                self.ts("dve", ev[:, 0:4], rv[:], -1.0, ip[:, 0:1], ALU.mult, ALU.add, [b1, b5], [b2])
                self.ts("dve", ev[:, 4:8], rv[:], ip[:, 1:2], None, ALU.subtract, None, [b1, b5, b2], [b2])
                self.ts("dve", msk[:], ev[:], 0.0, None, ALU.is_ge, None, [b2], [b3])
                self.ts("dve", ev[:], ev[:], 0.0, None, ALU.max, None, [b2], [b2])
                for d_ in range(2):
                    for r in range(4):
                        k = d_ * 4 + r
                        self.ts("dve", coef[:, k, :], self.lg[:, d_ * 4:d_ * 4 + 4], ev[:, k:k + 1], float(TOK),
                                ALU.mult, ALU.mult, [self.bLg, b2, b4], [b4])
                self.act(coef[:], coef[:], AF.Exp, [b4], [b4])
                for k in range(8):
                    self.ts("dve", coef[:, k, :], coef[:, k, :], msk[:, k:k + 1], None, ALU.mult, None, [b4, b3], [b4])
                FA = self.sb(tst, [128, 4, 256], F32, n=2)
                for k in range(8):
                    d_, h = k // 4, k % 4
                    fa, bfa = FA.next()
                    self.dma("sp", fa[:], bass.AP(self.st_all, k * 128 * 256, [[256, 128], [8 * 128 * 256, 4], [1, 256]]),
                             [self.b_st_all], [bfa])
                    for r in range(4):
                        cf = coef[:, d_ * 4 + r, h:h + 1]
                        if r == 0:
                            self.ts("dve", S_in[:, k, :], fa[:, 0, :], cf, None, ALU.mult, None, [bfa, b4], [bS[k]])
                        else:
                            self.stt("dve", S_in[:, k, :], fa[:, r, :], cf, S_in[:, k, :], ALU.mult, ALU.add, [bfa, b4, bS[k]], [bS[k]])
                KWB, bKWB = self.load_tiles_split(tst, self.KWB, NT, 512, self.bKWB, order="desc")
                psU = [self.ps(tst, [128, 512], F32) for _ in range(2)]
                pU, bpU = [p[0] for p in psU], [p[1] for p in psU]
                for c in reversed(range(NT)):
                    sbt, bsbt = SBT.next()
                    self.cp("act", sbt[:], S_in[:, 4:8, :].rearrange("p h e -> p (h e)"), bS[4:8], [bsbt])
                    self.dma("pool", self.SBd.ap()[c], sbt[:], [bsbt], [self.bSBd[c]])
                    self.state_step(pU, bpU, lambda h, c=c: KWB[:, c, h * 128:(h + 1) * 128], bKWB[c],
                                    lambda h, c=c: V[:, c, h * 256:(h + 1) * 256], bV[c], S_in, bS, self.gC, 4)
                self.P.flush()
            DT, bDT = self.sb(st, [128, 4, 128], F32)
            WQ, bWQ = self.sb(st, [128, 8, 128], F32)
            with nullcontext(st) as tst:
                di, b0 = self.sb(tst, [128, 128], I32)
                df, b1 = self.sb(tst, [128, 128], F32)
                pos, b2 = self.sb(tst, [128, 128], F32)
                neg, b3 = self.sb(tst, [128, 128], F32)
                arg, b4 = self.sb(tst, [128, 128], F32)
                r1i, b5 = self.sb(tst, [128, 128], I32)
                r1, b6 = self.sb(tst, [128, 128], F32)
                r2, b7 = self.sb(tst, [128, 128], F32)
                self.iota(di[:], [[1, 128]], 0, -1, [b0])
                self.cp("dve", df[:], di[:], [b0], [b1])
                self.ts("dve", pos[:], df[:], 0.0, None, ALU.max, None, [b1], [b2])
                self.tt("dve", neg[:], pos[:], df[:], ALU.subtract, [b1, b2], [b3])
                for h in range(4):
                    self.ts("dve", arg[:], pos[:], self.lg[:, h:h + 1], None, ALU.mult, None, [b2, self.bLg, b4], [b4])
                    self.stt("dve", arg[:], neg[:], self.lg[:, 4 + h:5 + h], arg[:], ALU.mult, ALU.add, [b3, self.bLg, b4], [b4])
                    self.act(DT[:, h, :], arg[:], AF.Exp, [b4], [bDT])
                self.ts("dve", DT[:], DT[:], SC, None, ALU.mult, None, [bDT], [bDT])
                self.iota(r1i[:], [[1, 128]], 1, 0, [b5])
                self.cp("dve", r1[:], r1i[:], [b5], [b6])
                self.ts("dve", r2[:], r1[:], -1.0, 129.0, ALU.mult, ALU.add, [b6], [b7])
                for h in range(4):
                    self.act(WQ[:, h, :], r1[:], AF.Exp, [b6, self.bLg], [bWQ], scale=self.lg[:, h:h + 1])
                    self.act(WQ[:, 4 + h, :], r2[:], AF.Exp, [b7, self.bLg], [bWQ], scale=self.lg[:, 4 + h:5 + h])
                self.ts("dve", WQ[:], WQ[:], SC, None, ALU.mult, None, [bWQ], [bWQ])
                pass
            stg = self.sb(st, [128, 512], F32, n=6)
            Wo, bWo = self.load_w(st, self.w_out_even, 0, 2048, D, stg)
            state, bstate = S_in, bS
            VT = self.sb(st, [128, 1024], BF16, n=2)
            SBt = self.sb(st, [128, 1024], BF16, n=2)
            KWt = self.sb(st, [128, 512], BF16, n=2)
            QT = self.sb(st, [128, 512], BF16, n=2)
            KT = self.sb(st, [128, 512], BF16, n=2)
            G = self.sb(st, [128, D], BF16, n=2)
            X = self.sb(st, [128, D], F32, n=3)
            YB = self.sb(st, [128, D], BF16, n=3)
            AT, bAT = self.sb(st, [128, 512], BF16)
            QF, bQF = self.sb(st, [128, 512], BF16)
            QB, bQB = self.sb(st, [128, 512], BF16)
            SF, bSF = self.sb(st, [128, 1024], BF16)
            junk, bj = self.sb(st, [128, D], F32)
            st4, bst4 = self.sb(st, [128, 16], F32)
            t1, bt1 = self.sb(st, [128, D], F32)
            ya, bya = self.sb(st, [128, D], BF16)
            yaT, byaT = self.sb(st, [128, D], BF16)
            XN = self.sb(st, [128, D], F32, n=2)
            psS, bpsS = self.ps(st, [128, 512], F32)
            psY = [self.ps(st, [128, 512], F32) for _ in range(2)]
            psU = [self.ps(st, [128, 512], F32) for _ in range(2)]
            pU, bpU = [p[0] for p in psU], [p[1] for p in psU]
            pT, bpT = self.ps(st, [128, D], BF16)
            psO = [self.ps(st, [128, 512], F32) for _ in range(2)]
            YA = self.sb(st, [128, D], BF16, n=2)
            loaded = {}
            aux = {}

            def S(c):
                qt, bqt = QT.next()
                kt, bkt = KT.next()
                g, bg = G.next()
                xt, bx = X.next()
                yb, byb = YB.next()
                self.dma("sp", qt[:], self.QT_ret.ap()[c], [self.bQT_ret[c]], [bqt])
                self.dma("sp", kt[:], self.KT_ret.ap()[c], [self.bKT_ret[c]], [bkt])
                self.dma("sp", g[:], self.G_ret.ap()[c], [self.bG_ret[c]], [bg])
                self.dma("sp", xt[:], self.x_in.ap()[128 + c * 128:256 + c * 128, :], [], [bx])
                self.dma("sp", yb[:], self.YbT.ap()[c], [self.bYbT[c]], [byb])
                vt, bvt = VT.next()
                sbt, bsbt = SBt.next()
                kwt, bkwt = KWt.next()
                self.dma("sp", vt[:], self.V_ret.ap()[c], [self.bV_ret[c]], [bvt])
                self.dma("sp", sbt[:], self.SBd.ap()[c], [self.bSBd[c]], [bsbt])
                self.dma("sp", kwt[:], self.KWF.ap()[c], [self.bKWF[c]], [bkwt])
                aux[c] = (vt, bvt, sbt, bsbt, kwt, bkwt)
                for h in range(4):
                    self.mm(psS[:, h * 128:(h + 1) * 128], kt[:, h * 128:(h + 1) * 128], qt[:, h * 128:(h + 1) * 128],
                            True, True, [bkt, bqt], [bpsS])
                loaded[c] = (qt, bqt, g, bg, xt, bx, yb, byb)

            def A1(c):
                qt, bqt, g, bg, xt, bx, yb, byb = loaded[c]
                self.tt("dve", AT[:], psS[:], DT[:].rearrange("p h r -> p (h r)"), ALU.mult, [bpsS, bDT], [bAT])
                self.tt("pool", QF[:], qt[:], WQ[:, 0:4, :].rearrange("p h r -> p (h r)"), ALU.mult, [bqt, bWQ], [bQF])
                self.tt("pool", QB[:], qt[:], WQ[:, 4:8, :].rearrange("p h r -> p (h r)"), ALU.mult, [bqt, bWQ], [bQB])
                self.cp("act", SF[:], state[:, 0:4, :].rearrange("p h e -> p (h e)"), bstate[0:4], [bSF])

            def A2(c):
                qt, bqt, g, bg, xt, bx, yb, byb = loaded[c]
                vt, bvt, sbt, bsbt, kwt, bkwt = aux.pop(c)
                if loader is not None:
                    loader.engs = ("act",)
                    loader.step(2)
                for h in range(4):
                    py, bpy = psY[h // 2]
                    yh = py[:, (h % 2) * 256:(h % 2 + 1) * 256]
                    self.mm(yh, AT[:, h * 128:(h + 1) * 128], vt[:, h * 256:(h + 1) * 256], True, False, [bAT, bvt], [bpy])
                    self.mm(yh, QF[:, h * 128:(h + 1) * 128], SF[:, h * 256:(h + 1) * 256], False, False, [bQF, bSF], [bpy])
                    self.mm(yh, QB[:, h * 128:(h + 1) * 128], sbt[:, h * 256:(h + 1) * 256], False, True, [bQB, bsbt], [bpy])
                self.state_step(pU, bpU, lambda h: kwt[:, h * 128:(h + 1) * 128], bkwt,
                                lambda h: vt[:, h * 256:(h + 1) * 256], bvt, state, bstate, self.gC, 0)
                if c >= 1:
                    Btr(c - 1)
                for b_ in range(2):
                    py, bpy = psY[b_]
                    self.act(junk[:, b_ * 512:(b_ + 1) * 512], py[:], AF.Square, [bpy], [bj])
                self.rsum(st4[:, 0:4], junk[:].rearrange("p (h e) -> p h e", h=4), [bj], [bst4])
                self.rstd(st4, bst4, 4, 1.0 / 256)
                for b_ in range(2):
                    py, bpy = psY[b_]
                    self.tt("dve", t1[:, b_ * 512:(b_ + 1) * 512].rearrange("p (h e) -> p h e", h=2),
                            py[:].rearrange("p (h e) -> p h e", h=2),
                            st4[:, 12 + 2 * b_:14 + 2 * b_].unsqueeze(2).broadcast_to([128, 2, 256]), ALU.mult,
                            [bpy, bst4], [bt1])
                ya, bya = YA.next()
                self.tt("dve", ya[:], t1[:], g[:], ALU.mult, [bt1, bg], [bya])
                loaded[c] = loaded[c] + (ya, bya)

            def Btr(c):
                ya, bya = loaded[c][8], loaded[c][9]
                for kc in range(8):
                    self.tr(pT[:, kc * 128:(kc + 1) * 128], ya[:, kc * 128:(kc + 1) * 128], [bya], [bpT])
                self.cp("act", yaT[:], pT[:], [bpT], [byaT])

            def Bout(c):
                qt, bqt, g, bg, xt, bx, yb, byb, ya, bya = loaded.pop(c)
                xn, bxn = XN.next()
                for dg in range(2):
                    po, bpo = psO[dg]
                    for kc in range(16):
                        lhsT = yaT[:, kc * 128:(kc + 1) * 128] if kc < 8 else yb[:, (kc - 8) * 128:(kc - 7) * 128]
                        self.mm(po[:], lhsT, Wo[:, kc, dg * 512:(dg + 1) * 512], kc == 0, kc == 15,
                                [byaT, byb, bWo(kc, dg * 512)], [bpo])
                    self.tt("dve", xn[:, dg * 512:(dg + 1) * 512], po[:], xt[:, dg * 512:(dg + 1) * 512], ALU.add,
                            [bpo, bx], [bxn])
                self.dma("pool", self.xres.ap()[c], xn[:], [bxn], [self.bxres[c]])

            S(0)
            A1(0)
            for c in range(NT):
                A2(c)
                if c + 1 < NT:
                    S(c + 1)
                    A1(c + 1)
                if c >= 1:
                    Bout(c - 1)
            Btr(NT - 1)
            Bout(NT - 1)
            if loader is not None:
                loader.finish()
            self.P.flush()

    def phase_mlp(self, layer, src, bsrc, dst_ap_fn, bdst, pre_up=None, stg=None):
        with ExitStack() as st:
            if stg is None:
                stg = self.sb(st, [128, 512], F32, n=6)
            if pre_up is not None:
                Wu, bWu = pre_up.w, pre_up.wb
            else:
                Wu, bWu = self.load_w(st, self.w_up, layer * D, D, 4096, stg)
            Wd, bWd = self.load_w(st, self.w_down, layer * 4096, 4096, D, stg)
            Ld = self.last_loader
            gain, bg = self.sb(st, [128, D], F32)
            self.dma("sp", gain[:], self.bcast_row(self.norm_mlp, layer * D, D), [], [bg])
            X = self.sb(st, [128, D], F32, n=2)
            junk, bj = self.sb(st, [128, D], BF16)
            ST = self.sb(st, [128, 16], F32, n=2)
            HB = self.sb(st, [128, D], BF16, n=2)
            HT = self.sb(st, [128, D], BF16, n=2)
            R = self.sb(st, [128, 512], F32, n=2)
            U, bU = self.sb(st, [128, 4096], BF16)
            UT, bUT = self.sb(st, [128, 4096], BF16)
            XN = self.sb(st, [128, D], F32, n=2)
            pT, bpT = self.ps(st, [128, D], BF16)
            PU = self.ps(st, [128, 512], F32, n=3)
            PT = self.ps(st, [128, D], BF16, n=2)
            psO = [self.ps(st, [128, 512], F32) for _ in range(2)]
            bUTq = [Buf() for _ in range(4)]
            bUq = [Buf() for _ in range(4)]

            def start_norm(t):
                xt, bx = X.next()
                stt_, bst = ST.next()
                hb, bhb = HB.next()
                self.dma("sp", xt[:], src.ap()[t], [bsrc[t]], [bx])
                self.norm_part(xt, bx, gain, bg, junk, bj, stt_, bst, hb, bhb)
                return xt, bx, hb, bhb

            def start_tr(xt, bx, hb, bhb):
                hT, bhT = HT.next()
                self.tr_part(hb, bhb, pT, bpT, hT, bhT)
                return xt, bx, hT, bhT

            cur = start_tr(*start_norm(0))
            for t in range(NT):
                xt, bx, hT, bhT = cur
                pipe = Pipe()
                for fg in range(8):
                    pu, bpu = PU.next()
                    for kc in range(8):
                        self.mm(pu[:], hT[:, kc * 128:(kc + 1) * 128], Wu[:, kc, fg * 512:(fg + 1) * 512],
                                kc == 0, kc == 7, [bhT, bWu(kc, fg * 512)], [bpu])
                    r, br = R.next()
                    self.act(r[:], pu[:], AF.Relu, [bpu], [br])
                    self.tt("dve", U[:, fg * 512:(fg + 1) * 512], r[:], r[:], ALU.mult, [br], [bUq[fg // 2]])
                    if fg % 2 == 1:
                        def trs(q4=fg // 2):
                            pt, bpt = PT.next()
                            for i in range(8):
                                fc = q4 * 8 + i
                                self.tr(pt[:, i * 128:(i + 1) * 128], U[:, fc * 128:(fc + 1) * 128], [bUq[q4]], [bpt])
                            self.cp("act" if q4 % 2 else "dve", UT[:, q4 * 1024:(q4 + 1) * 1024], pt[:], [bpt], [bUTq[q4]])
                        pipe.defer(3, trs)
                    if fg == 0 and t + 1 < NT:
                        nh = start_norm(t + 1)
                    if t == 0:
                        Ld.step(8)
                    pipe.tick()
                if t + 1 < NT:
                    nxt = start_tr(*nh)
                pipe.drain()
                xn, bxn = XN.next()
                for dg in range(2):
                    po, bpo = psO[dg]
                    for fc in range(32):
                        self.mm(po[:], UT[:, fc * 128:(fc + 1) * 128], Wd[:, fc, dg * 512:(dg + 1) * 512],
                                fc == 0, fc == 31, [bUTq[fc // 8], bWd(fc, dg * 512)], [bpo])
                    self.tt("dve", xn[:, dg * 512:(dg + 1) * 512], po[:], xt[:, dg * 512:(dg + 1) * 512], ALU.add,
                            [bpo, bx], [bxn])
                self.dma("pool", dst_ap_fn(t), xn[:], [bxn], [bdst[t]])
                cur = nxt if t + 1 < NT else None
            self.P.flush()

    def phase_E(self):
        self.QT1 = self.dram("QT1", [NT, 128, 1024], BF16)
        self.bQT1 = [Buf() for _ in range(NT)]
        self.k_src = self.dram("k_src", [256, TOK], BF16)
        self.k_all = self.dram("k_all", [4 * 256, TOK], BF16)
        self.v_src = self.dram("v_src", [256, TOK], BF16)
        self.v_all = self.dram("v_all", [4 * 256, TOK], BF16)
        self.b_k_src, self.b_k_all, self.b_v_src, self.b_v_all = Buf(), Buf(), Buf(), Buf()
        with ExitStack() as st:
            stg = self.sb(st, [128, 512], F32, n=6)
            W, bW = self.load_w(st, self.w_in_odd, 0, D, 1536, stg)
            gain, bg = self.sb(st, [128, D], F32)
            gq, bgq = self.sb(st, [128, 128], F32)
            gk, bgk = self.sb(st, [128, 128], F32)
            self.dma("sp", gain[:], self.bcast_row(self.norm_mix, D, D), [], [bg])
            self.dma("sp", gq[:], self.bcast_row(self.ax_q_norm, 0, 128), [], [bgq])
            self.dma("sp", gk[:], self.bcast_row(self.ax_k_norm, 0, 128), [], [bgk])
            self.ts("dve", gq[:], gq[:], 128.0 ** -0.5, None, ALU.mult, None, [bgq], [bgq])
            TBL = {}
            for nm in ("Cq", "Sq", "Ck", "Sk"):
                TBL[nm] = self.sb(st, [128, NT, 128], F32)
            with ExitStack() as tst:
                C4, bC4 = self.sb(tst, [128, NT, 128], F32)
                S4, bS4 = self.sb(tst, [128, NT, 128], F32)
                gqs, bgqs = self.sb(tst, [128, 128], F32)
                gks, bgks = self.sb(tst, [128, 128], F32)
                inv, binv = self.sb(tst, [128, 32], F32)
                rowi, b0 = self.sb(tst, [128, NT], I32)
                rowf, b1 = self.sb(tst, [128, NT], F32)
                pge, b2 = self.sb(tst, [128, 2], F32)
                ang, bang = self.sb(tst, [128, NT, 2, 32], F32)
                cs, bcs = self.sb(tst, [128, NT, 2, 32], F32)
                sn, bsn = self.sb(tst, [128, NT, 2, 32], F32)
                for j in range(32):
                    self.memset("pool", inv[:, j:j + 1], float(np.float32(10000.0) ** np.float32(-2.0 * j / 64.0)), [binv])
                self.iota(rowi[:], [[2, NT]], 0, 0, [b0])
                self.cp("dve", rowf[:], rowi[:], [b0], [b1])
                self.ts("dve", pge[:, 0:1], self.pcol[:], 64.0, None, ALU.is_ge, None, [self.bPcol], [b2])
                self.stt("dve", pge[:, 1:2], pge[:, 0:1], -64.0, self.pcol[:], ALU.mult, ALU.add, [b2, self.bPcol], [b2])
                self.stt("dve", pge[:, 0:1], self.meta[:, 0:1], 1.0 / 64, pge[:, 0:1], ALU.mult, ALU.add, [self.bMeta, b2], [b2])
                self.ts("dve", rowf[:], rowf[:], pge[:, 0:1], None, ALU.add, None, [b1, b2], [b1])
                self.tt("dve", ang[:, :, 0, :], inv[:].unsqueeze(1).broadcast_to([128, NT, 32]),
                        rowf[:].unsqueeze(2).broadcast_to([128, NT, 32]), ALU.mult, [binv, b1], [bang])
                self.ts("dve", ang[:, :, 1, :], inv[:].unsqueeze(1).broadcast_to([128, NT, 32]), pge[:, 1:2], None,
                        ALU.mult, None, [binv, b2, bang], [bang])
                self.sincos(tst, ang[:], bang, [128, NT, 2, 32], cs[:], sn[:], [bcs, bsn])
                C4v = C4[:].rearrange("p t (k two d) -> p t k two d", k=2, two=2)
                S4v = S4[:].rearrange("p t (k two d) -> p t k two d", k=2, two=2)
                for hf in range(2):
                    self.cp("dve", C4v[:, :, :, hf, :], cs[:], [bcs, bsn], [bC4])
                self.cp("dve", S4v[:, :, :, 1, :], sn[:], [bcs, bsn], [bS4])
                self.ts("dve", S4v[:, :, :, 0, :], sn[:], -1.0, None, ALU.mult, None, [bcs, bsn, bS4], [bS4])
                for g_, gs_, bg_, bgs_, cn, sn_ in ((gq, gqs, bgq, bgqs, "Cq", "Sq"), (gk, gks, bgk, bgks, "Ck", "Sk")):
                    self.cp("dve", gs_[:].rearrange("p (k two d) -> p k two d", k=2, two=2),
                            g_[:].rearrange("p (k two d) -> p k two d", k=2, two=2)[:, :, ::-1, :], [bg_], [bgs_])
                    self.tt("dve", TBL[cn][0][:], C4[:], g_[:].unsqueeze(1).broadcast_to([128, NT, 128]), ALU.mult,
                            [bC4, bg_], [TBL[cn][1]])
                    self.tt("dve", TBL[sn_][0][:], S4[:], gs_[:].unsqueeze(1).broadcast_to([128, NT, 128]), ALU.mult,
                            [bS4, bgs_], [TBL[sn_][1]])
                self.P.flush()
            X = self.sb(st, [128, D], F32, n=2)
            junk, bj = self.sb(st, [128, D], BF16)
            ST = self.sb(st, [128, 16], F32, n=2)
            HB = self.sb(st, [128, D], BF16, n=2)
            HT = self.sb(st, [128, D], BF16, n=2)
            pT, bpT = self.ps(st, [128, D], BF16)
            PS = self.ps(st, [128, 512], F32, n=4)
            PQ = self.ps(st, [128, 512], BF16, n=2)
            TA = self.sb(st, [128, 512], F32, n=4)
            TB = self.sb(st, [128, 512], F32, n=4)
            QN = self.sb(st, [128, 512], F32, n=3)
            ST4 = self.sb(st, [128, 16], F32, n=4)
            QR = self.sb(st, [128, 512], BF16, n=8)
            QS = self.sb(st, [128, D], BF16, n=4)
            KS = self.sb(st, [128, 256], BF16, n=2)
            VS = self.sb(st, [128, 256], BF16, n=2)

            def headnorm(ps_ap, bps, n, g, bgn, dst_ap, bdst):
                st4, bst4 = ST4.next()
                ta, bta = TA.next()
                tb, btb = TB.next()
                self.act(ta[:, 0:n * 128], ps_ap, AF.Square, [bps], [bta])
                self.rsum(st4[:, 0:n], ta[:, 0:n * 128].rearrange("p (h d) -> p h d", h=n), [bta], [bst4])
                self.rstd(st4, bst4, n, 1.0 / 128)
                self.tt("dve", dst_ap.rearrange("p (h d) -> p h d", h=n),
                        ps_ap.rearrange("p (h d) -> p h d", h=n),
                        st4[:, 12:12 + n].unsqueeze(2).broadcast_to([128, n, 128]), ALU.mult, [bps, bst4], [bdst])

            def rope(src, bsrc, n, t, dst, bdst, cn="Cq", sn_="Sq"):
                C4, bC4 = TBL[cn]
                S4, bS4 = TBL[sn_]
                ta, bta = TA.next()
                tb, btb = TB.next()
                for k in range(2):
                    sv = src[:, 0:n * 128].rearrange("p (h k two d) -> p h k two d", h=n, k=2, two=2)[:, :, k, :, :]
                    av = ta[:, 0:n * 128].rearrange("p (h k two d) -> p h k two d", h=n, k=2, two=2)[:, :, k, :, :]
                    bv = tb[:, 0:n * 128].rearrange("p (h k two d) -> p h k two d", h=n, k=2, two=2)[:, :, k, :, :]
                    Cb = C4[:, t, k * 64:(k + 1) * 64].rearrange("p (two d) -> p two d", two=2).unsqueeze(1).broadcast_to([128, n, 2, 32])
                    Sb = S4[:, t, k * 64:(k + 1) * 64].rearrange("p (two d) -> p two d", two=2).unsqueeze(1).broadcast_to([128, n, 2, 32])
                    self.tt("dve", av, sv, Cb, ALU.mult, [bsrc, bC4], [bta])
                    self.tt("dve", bv, sv[:, :, ::-1, :], Sb, ALU.mult, [bsrc, bS4], [btb])
                self.tt("pool", dst[:, 0:n * 128], ta[:, 0:n * 128], tb[:, 0:n * 128], ALU.add, [bta, btb], [bdst])

            def transposes(src, bsrc, n, dst_ap, bdst):
                pq, bpq = PQ.next()
                for h in range(n):
                    self.tr(pq[:, h * 128:(h + 1) * 128], src[:, h * 128:(h + 1) * 128], [bsrc], [bpq])
                self.cp("act", dst_ap, pq[:, 0:n * 128], [bpq], [bdst])

            pipe = Pipe()

            def start_norm(t):
                xt, bx = X.next()
                stt_, bst = ST.next()
                hb, bhb = HB.next()
                self.dma("sp", xt[:], self.xres2.ap()[t], [self.bxres2[t]], [bx])
                self.norm_part(xt, bx, gain, bg, junk, bj, stt_, bst, hb, bhb)
                return hb, bhb

            def start_tr(hb, bhb):
                hT, bhT = HT.next()
                self.tr_part(hb, bhb, pT, bpT, hT, bhT)
                return hT, bhT

            HTA, _ = self.sb(st, [128, NT, D], BF16)
            bHTA = [Buf() for _ in range(NT)]

            def start_tr_all(t, hb, bhb):
                self.tr_part(hb, bhb, pT, bpT, HTA[:, t, :], bHTA[t])

            def kv_group(t):
                hT, bhT = HTA[:, t, :], bHTA[t]
                ps, bps = PS.next()
                for kc in range(8):
                    self.mm(ps[:], hT[:, kc * 128:(kc + 1) * 128], W[:, kc, 1024:1536],
                            kc == 0, kc == 7, [bhT, bW(kc, 1024)], [bps])
                qn, bqn = QN.next()
                qr, bqr = QR.next()
                headnorm(ps[:, 0:256], bps, 2, gk, bgk, qn[:, 0:256], bqn)
                rope(qn, bqn, 2, t, qr, bqr, "Ck", "Sk")

                def stage2(qr=qr, bqr=bqr, t=t):
                    ks, bks = KS.next()
                    transposes(qr, bqr, 2, ks[:], bks)
                    self.dma("pool", bass.AP(self.k_src, t * 128, [[TOK, 128], [128 * TOK, 2], [1, 128]]),
                             ks[:].rearrange("p (k q) -> p k q", k=2), [bks], [self.b_k_src])
                pipe.defer(3, stage2)
                vs, bvs = VS.next()
                self.cp("act", vs[:], ps[:, 256:512], [bps], [bvs])
                self.dma("pool", bass.AP(self.v_src, t * 128 * 256, [[256, 128], [1, 256]]),
                         vs[:], [bvs], [self.b_v_src])

            start_tr_all(0, *start_norm(0))
            nh = start_norm(1)
            for t in range(NT):
                kv_group(t)
                if t + 1 < NT:
                    start_tr_all(t + 1, *nh)
                    if t + 2 < NT:
                        nh = start_norm(t + 2)
                pipe.tick()
            pipe.drain()
            ksrc, kdst = self.k_src.ap(), self.k_all.ap()
            vsrc, vdst = self.v_src.ap(), self.v_all.ap()
            self.P.cc(lambda e: e.collective_compute("AllGather", ALU.bypass, replica_groups=GROUPS,
                                                     ins=[ksrc], outs=[kdst]), [self.b_k_src], [self.b_k_all])
            self.P.cc(lambda e: e.collective_compute("AllGather", ALU.bypass, replica_groups=GROUPS,
                                                     ins=[vsrc], outs=[vdst]), [self.b_v_src], [self.b_v_all])
            for t in range(NT):
                hT, bhT = HTA[:, t, :], bHTA[t]
                qs, bqs = QS.next()
                for g in range(2):
                    ps, bps = PS.next()
                    for kc in range(8):
                        self.mm(ps[:], hT[:, kc * 128:(kc + 1) * 128], W[:, kc, g * 512:(g + 1) * 512],
                                kc == 0, kc == 7, [bhT, bW(kc, g * 512)], [bps])
                    qn, bqn = QN.next()
                    qr, bqr = QR.next()
                    headnorm(ps[:], bps, 4, gq, bgq, qn[:], bqn)
                    rope(qn, bqn, 4, t, qr, bqr)

                    def stage2(g=g, qr=qr, bqr=bqr, qs=qs, bqs=bqs, t=t):
                        transposes(qr, bqr, 4, qs[:, g * 512:(g + 1) * 512], bqs)
                        if g == 1:
                            self.dma("pool", self.QT1.ap()[t], qs[:], [bqs], [self.bQT1[t]])
                    pipe.defer(4, stage2)
                    pipe.tick()
            pipe.drain()
            self.P.flush()

    def phase_F(self, loaders=(), wo=None):
        self.YT1 = self.dram("YT1", [NT, 128, 1024], BF16)
        self.bYT1 = [Buf() for _ in range(NT)]
        if wo is not None:
            self.xres3 = self.dram("xres3", [NT, 128, D], F32)
            self.bxres3 = [Buf() for _ in range(NT)]
        LA = 2
        VW = 132
        with ExitStack() as st:
            KT, _ = self.sb(st, [128, 2, 4 * TOK], BF16)
            V, _ = self.sb(st, [128, 64, 2, VW], BF16)
            bKTr = [Buf() for _ in range(4)]
            bVr = [[Buf(), Buf()] for _ in range(4)]
            for r in range(4):
                for kvh in range(2):
                    self.memset("pool", V[:, r * NT:(r + 1) * NT, kvh, 128:VW], 1.0, [bVr[r][kvh]])
            for r in range(4):
                self.dma("sp", KT[:, :, r * TOK:(r + 1) * TOK],
                         bass.AP(self.k_all, r * 256 * TOK, [[TOK, 128], [128 * TOK, 2], [1, TOK]]), [self.b_k_all], [bKTr[r]])
                for kvh in range(2):
                    self.dma("act", V[:, r * NT:(r + 1) * NT, kvh, 0:128],
                             bass.AP(self.v_all, r * 256 * TOK + kvh * 128, [[256, 128], [128 * 256, NT], [1, 128]]),
                             [self.b_v_all], [bVr[r][kvh]])
            QT = self.sb(st, [128, 1024], BF16, n=2)
            YTK = self.sb(st, [128, 512], BF16, n=2)
            YB = self.sb(st, [128, 1024], BF16, n=2)
            RD = self.sb(st, [128, 8], F32, n=2)
            OS = self.sb(st, [128, 4, 132], F32, n=2)
            OSB = [[Buf() for _ in range(4)] for _ in range(2)]
            PSS = self.ps(st, [128, 512], F32, n=3)
            PO4 = [self.ps(st, [128, 512], F32) for _ in range(4)]
            PTf, bPTf = self.ps(st, [128, 512], F32)
            PTb = PTf[:].bitcast(BF16)
            XR = self.sb(st, [128, D], F32, n=2)
            XN = self.sb(st, [128, D], F32, n=2)
            xrs = {}
            Pb = self.sb(st, [128, 512], BF16, n=7)
            its = [(t, kvh, jt) for t in range(NT) for kvh in range(2) for jt in range(64)]
            qts, ybs, units = {}, {}, {}
            pend = []
            tails = []

            def front(t, kvh, jt):
                if kvh == 0 and jt == 0:
                    qt, bqt = QT.next()
                    self.dma("sp", qt[:], self.QT1.ap()[t], [self.bQT1[t]], [bqt])
                    qts[t] = (qt, bqt)
                    ybs[t] = YB.next()
                    if wo is not None:
                        xr, bxr = XR.next()
                        self.dma("sp", xr[:], self.xres2.ap()[t], [self.bxres2[t]], [bxr])
                        xrs[t] = (xr, bxr)
                qt, bqt = qts[t]
                sT, bsT = PSS.next()
                self.mm(sT[:], KT[:, kvh, jt * 128:(jt + 1) * 128], qt[:, kvh * 512:(kvh + 1) * 512],
                        True, True, [bKTr[jt // 16], bqt], [bsT])
                p_, bp = Pb.next()
                self.act(p_[:], sT[:], AF.Exp, [bsT], [bp])
                return p_, bp

            def back(i, t, kvh, jt, p_, bp):
                for h in range(4):
                    o, bo = PO4[h]
                    self.mm(o[:, 0:129], p_[:, h * 128:(h + 1) * 128], V[:, jt, kvh, 0:129],
                            jt == 0, jt == 63, [bVr[jt // 16][kvh], bp], [bo])
                if jt == 63:
                    rd, brd = RD.next()
                    ytk, bytk = YTK.next()
                    yb, byb = ybs[t]
                    os_, _ = OS.next()
                    bosh = OSB[OS.i % 2]
                    for h in range(4):
                        o, bo = PO4[h]
                        self.cp("dve" if h < 3 else "act", os_[:, h, 0:129], o[:, 0:129], [bo], [bosh[h]])
                    self.recip(rd[:, 0:4], os_[:, :, 128], bosh, [brd])
                    for h in range(4):
                        self.ts("dve", ytk[:, h * 128:(h + 1) * 128], os_[:, h, 0:128], rd[:, h:h + 1], None,
                                ALU.mult, None, [bosh[h], brd], [bytk])

                    def tail(ytk=ytk, bytk=bytk, yb=yb, byb=byb, kvh=kvh, t=t):
                        for h in range(4):
                            self.tr(PTb[:, h * 128:(h + 1) * 128], ytk[:, h * 128:(h + 1) * 128], [bytk], [bPTf])
                        self.cp("dve", yb[:, kvh * 512:(kvh + 1) * 512], PTb[:, 0:512], [bPTf], [byb])
                        if kvh == 1 and wo is None:
                            self.dma("pool", self.YT1.ap()[t], yb[:], [byb], [self.bYT1[t]])
                    tails.append((i + 4, tail))
                    if kvh == 1 and wo is not None:
                        xn, bxn = XN.next()

                        def oproj(dg, yb=yb, byb=byb, t=t, xn=xn, bxn=bxn):
                            xr, bxr = xrs[t]
                            for kc in range(8):
                                self.mm(PTf[:], yb[:, kc * 128:(kc + 1) * 128], wo.w[:, kc, dg * 512:(dg + 1) * 512],
                                        kc == 0, kc == 7, [byb, wo.wb(kc, dg * 512)], [bPTf])
                            self.tt("dve", xn[:, dg * 512:(dg + 1) * 512], PTf[:], xr[:, dg * 512:(dg + 1) * 512], ALU.add,
                                    [bPTf, bxr], [bxn])
                            if dg == 1:
                                self.dma("pool", self.xres3.ap()[t], xn[:], [bxn], [self.bxres3[t]])
                        tails.append((i + 10, lambda f=oproj: f(0)))
                        tails.append((i + 16, lambda f=oproj: f(1)))

            n = len(its)
            for i in range(n + LA + 24):
                if i < n:
                    pend.append(front(*its[i]))
                if LA <= i < n + LA:
                    p_, bp = pend[i - LA]
                    back(i, *its[i - LA], p_, bp)
                while tails and tails[0][0] <= i:
                    tails.pop(0)[1]()
                if i % 16 == 8:
                    for L in loaders:
                        if L.todo:
                            L.step(1)
                            break
            assert not tails
            for L in loaders:
                L.finish()
            self.P.flush()

    def phase_G(self, pre=None):
        self.xres3 = self.dram("xres3", [NT, 128, D], F32)
        self.bxres3 = [Buf() for _ in range(NT)]
        with ExitStack() as st:
            if pre is not None:
                Wo, bWo = pre.w, pre.wb
            else:
                stg = self.sb(st, [128, 512], F32, n=6)
                Wo, bWo = self.load_w(st, self.w_out_odd, 0, D, D, stg)
            X = self.sb(st, [128, D], F32, n=2)
            YB = self.sb(st, [128, D], BF16, n=2)
            XN = self.sb(st, [128, D], F32, n=2)
            PO = self.ps(st, [128, 512], F32, n=4)
            for t in range(NT):
                xt, bx = X.next()
                yb, byb = YB.next()
                xn, bxn = XN.next()
                self.dma("sp", xt[:], self.xres2.ap()[t], [self.bxres2[t]], [bx])
                self.dma("sp", yb[:], self.YT1.ap()[t], [self.bYT1[t]], [byb])
                for dg in range(2):
                    po, bpo = PO.next()
                    for kc in range(8):
                        self.mm(po[:], yb[:, kc * 128:(kc + 1) * 128], Wo[:, kc, dg * 512:(dg + 1) * 512], kc == 0, kc == 7,
                                [byb, bWo(kc, dg * 512)], [bpo])
                    self.tt("dve", xn[:, dg * 512:(dg + 1) * 512], po[:], xt[:, dg * 512:(dg + 1) * 512], ALU.add,
                            [bpo, bx], [bxn])
                self.dma("pool", self.xres3.ap()[t], xn[:], [bxn], [self.bxres3[t]])
            self.P.flush()

    def finish(self):
        self.P.drain()
        self.P.flush()

    def build(self):
        self.setup()
        if getattr(self, "only", None) == "E":
            self.xres2 = self.nc.dram_tensor("xdbg", [NT, 128, D], F32, kind="ExternalInput")
            self.bxres2 = [Buf() for _ in range(NT)]
            self.phase_E()
            return self.finish()
        self.phase_A()
        if self.stage <= 1:
            return self.finish()
        self.phase_B1()
        if self.stage <= 2:
            return self.finish()
        with ExitStack() as pre0:
            stgP = self.sb(pre0, [128, 512], F32, n=6)
            L0 = WLoader(self, pre0, self.w_up, 0, D, 4096, stgP, q="pool", engs=("pool",))
            self.phase_B2(loader=L0)
            if self.stage <= 3:
                L0.finish()
                return self.finish()
            self.phase_B3(loader=L0)
            if self.stage <= 4:
                return self.finish()
            self.xres2 = self.dram("xres2", [NT, 128, D], F32)
            self.bxres2 = [Buf() for _ in range(NT)]
            self.phase_mlp(0, self.xres, self.bxres, lambda t: self.xres2.ap()[t], self.bxres2, pre_up=L0, stg=stgP)
            if self.stage <= 5:
                return self.finish()
        self.phase_E()
        if self.stage <= 6:
            return self.finish()
        with ExitStack() as pre1:
            stgP = self.sb(pre1, [128, 512], F32, n=6)
            Lo = WLoader(self, pre1, self.w_out_odd, 0, D, D, stgP, q="pool", engs=("pool",))
            L1 = WLoader(self, pre1, self.w_up, D, D, 4096, stgP, q="pool", engs=("pool",))
            self.phase_F(loaders=[Lo, L1], wo=Lo)
            if self.stage <= 8:
                return self.finish()
            self.by = [Buf() for _ in range(NT)]
            self.phase_mlp(1, self.xres3, self.bxres3, lambda t: self.y_out.ap()[t * 128:(t + 1) * 128, :], self.by,
                           pre_up=L1, stg=stgP)
            return self.finish()


def host_inputs(inputs):
    x = np.ascontiguousarray(inputs["x"], dtype=np.float32)
    S = x.shape[1]
    maps = []
    shared = {
        "norm_mix": inputs["norm_mix"], "norm_mlp": inputs["norm_mlp"],
        "w_in_even": inputs["w_in_even"][0], "w_out_even": inputs["w_out_even"][0],
        "ret_decay_logit": inputs["ret_decay_logit"].reshape(1, 8), "ret_norm": inputs["ret_norm"],
        "swa_q_norm": inputs["swa_q_norm"], "swa_k_norm": inputs["swa_k_norm"],
        "swa_sink": inputs["swa_sink"], "t5_table": inputs["t5_table"],
        "w_in_odd": inputs["w_in_odd"][0], "w_out_odd": inputs["w_out_odd"][0],
        "ax_q_norm": inputs["ax_q_norm"], "ax_k_norm": inputs["ax_k_norm"],
        "w_mlp_up": inputs["w_mlp_up"], "w_mlp_down": inputs["w_mlp_down"],
    }
    shared = {k: np.ascontiguousarray(v, dtype=np.float32) for k, v in shared.items()}
    for c in range(8):
        b, q = c // 4, c % 4
        t0 = q * TOK
        xs = np.zeros((TOK + 256, D), np.float32)
        lo, hi = max(t0 - 128, 0), min(t0 + TOK + 128, S)
        xs[lo - (t0 - 128):hi - (t0 - 128)] = x[b, lo:hi]
        meta = np.zeros((128, 8), np.float32)
        meta[:, 0] = t0
        meta[:, 1] = q
        meta[:, 2] = 1.0 if q > 0 else 0.0
        meta[:, 3] = 1.0 if q < 3 else 0.0
        m = dict(shared)
        m["x"] = xs
        m["meta"] = meta
        maps.append(m)
    return maps


def kernel(**inputs):
    mk = MK()
    mk.build()
    res = run_bass_kernel_spmd(mk.nc, host_inputs(inputs), core_ids=list(range(8)))
    out = np.empty((2, 4 * TOK, D), np.float32)
    for c in range(8):
        out[c // 4, (c % 4) * TOK:(c % 4 + 1) * TOK] = res.results[c]["y"]
    return out
```
